# Optimizing a Trainium2 kernel written in Bass

```python
import math
import jax, jax.numpy as jnp
from jax import lax
import numpy as np

D_MODEL = 4096
BATCH = 1
SEQ = 8192
DEPTH = 2

N_MEM = 256
CONV_DIM = 2048
CONV_K = 3
N_HEADS = 16
N_KV_HEADS = 4
HEAD_DIM = 128
GROUP = N_HEADS // N_KV_HEADS
ATTN_DIM = N_HEADS * HEAD_DIM
KV_DIM = N_KV_HEADS * HEAD_DIM
IDX_HEADS = 32
IDX_DIM = 64
TOPK_MAX = 256
Q_BLOCK = 128
N_BUCKETS = 32
MAX_DISTANCE = 128
X_HEADS = 4
X_HEAD_DIM = 128
X_DIM = X_HEADS * X_HEAD_DIM
D_FF = 4 * D_MODEL
EPS = 1e-6

SPLITS = (CONV_DIM, CONV_DIM, CONV_DIM,
          ATTN_DIM, KV_DIM, KV_DIM,
          IDX_HEADS * IDX_DIM, IDX_DIM, IDX_HEADS,
          D_MODEL, D_MODEL)
IN_DIM = int(sum(SPLITS))
SPLIT_POINTS = tuple(int(s) for s in np.cumsum(SPLITS)[:-1])

kernel_name = "hybrid_shortconv_dsa_gated_trunk"


def rmsnorm(x, g):
    xf = x.astype(jnp.float32)
    inv = lax.rsqrt(jnp.mean(xf * xf, axis=-1, keepdims=True) + EPS)
    return (xf * inv).astype(x.dtype) * g


def t5_bucket(dist):
    n = jnp.maximum(dist, 0)
    max_exact = N_BUCKETS // 2
    nf = jnp.maximum(n, 1).astype(jnp.float32)
    large = max_exact + (jnp.log(nf / max_exact) / math.log(MAX_DISTANCE / max_exact)
                         * (N_BUCKETS - max_exact)).astype(jnp.int32)
    large = jnp.minimum(large, N_BUCKETS - 1)
    return jnp.where(n < max_exact, n, large)


def causal_dwconv(u, w):
    S = u.shape[1]
    up = jnp.pad(u, ((0, 0), (CONV_K - 1, 0), (0, 0)))
    z = w[CONV_K - 1] * u
    for j in range(CONV_K - 1):
        z = z + w[j] * up[:, j:j + S]
    return z


def dsa_attention(q, k, v, iq, ik, iw, rel_bias):
    B, S = q.shape[0], q.shape[1]
    topk = min(TOPK_MAX, S // 4)
    n_blocks = S // Q_BLOCK
    kpos = jnp.arange(S, dtype=jnp.int32)
    idx_scale = IDX_DIM ** -0.5
    attn_scale = HEAD_DIM ** -0.5
    gather = jax.vmap(lambda arr, ii: arr[ii])

    def block(i):
        start = i * Q_BLOCK
        qb = lax.dynamic_slice_in_dim(q, start, Q_BLOCK, axis=1)
        iqb = lax.dynamic_slice_in_dim(iq, start, Q_BLOCK, axis=1)
        iwb = lax.dynamic_slice_in_dim(iw, start, Q_BLOCK, axis=1)
        qpos = start + jnp.arange(Q_BLOCK, dtype=jnp.int32)
        causal = kpos[None, :] <= qpos[:, None]
        sc = jax.nn.relu(jnp.einsum('bqhd,bsd->bqhs', iqb, ik) * idx_scale)
        sc = jnp.einsum('bqh,bqhs->bqs', iwb, sc).astype(jnp.float32)
        sc = jnp.where(causal[None], sc, -jnp.inf)
        _, sel = lax.top_k(sc, topk)
        k_sel = gather(k, sel)
        v_sel = gather(v, sel)
        qg = qb.reshape(B, Q_BLOCK, N_KV_HEADS, GROUP, HEAD_DIM)
        logits = jnp.einsum('bqhgd,bqshd->bqhgs', qg, k_sel).astype(jnp.float32) * attn_scale
        dist = qpos[None, :, None] - sel
        bias = rel_bias[t5_bucket(dist)].astype(jnp.float32)
        bias = bias.reshape(B, Q_BLOCK, topk, N_KV_HEADS, GROUP).transpose(0, 1, 3, 4, 2)
        valid = (dist >= 0)[:, :, None, None, :]
        logits = jnp.where(valid, logits + bias, -jnp.inf)
        p = jax.nn.softmax(logits, axis=-1).astype(v.dtype)
        o = jnp.einsum('bqhgs,bqshd->bqhgd', p, v_sel)
        return o.reshape(B, Q_BLOCK, ATTN_DIM)

    out = lax.map(block, jnp.arange(n_blocks, dtype=jnp.int32))
    return out.transpose(1, 0, 2, 3).reshape(B, S, ATTN_DIM)


def memory_cross_attention(h, m, w_xq, w_xkv, w_xo):
    B, S = h.shape[0], h.shape[1]
    M = m.shape[1]
    qx = (h @ w_xq).reshape(B, S, X_HEADS, X_HEAD_DIM)
    kv = m @ w_xkv
    kx = kv[..., :X_DIM].reshape(B, M, X_HEADS, X_HEAD_DIM)
    vx = kv[..., X_DIM:].reshape(B, M, X_HEADS, X_HEAD_DIM)
    logits = jnp.einsum('bqhd,bmhd->bhqm', qx, kx).astype(jnp.float32) * (X_HEAD_DIM ** -0.5)
    p = jax.nn.softmax(logits, axis=-1).astype(vx.dtype)
    o = jnp.einsum('bhqm,bmhd->bqhd', p, vx).reshape(B, S, X_DIM)
    return o @ w_xo


def setup_inputs(seed: int = 0) -> dict:
    key = jax.random.key(seed)
    ks = jax.random.split(key, 20)

    def nrm(k, shape, scale):
        return jax.random.normal(k, shape, jnp.float32) * scale

    def gain(k, shape):
        return 1.0 + 0.02 * jax.random.normal(k, shape, jnp.float32)

    return {
        "x": nrm(ks[0], (BATCH, SEQ, D_MODEL), 1.0),
        "mem": nrm(ks[1], (BATCH, N_MEM, D_MODEL), 1.0),
        "rel_bias": nrm(ks[2], (N_BUCKETS, N_HEADS), 0.5),
        "norm_mix": gain(ks[3], (DEPTH, D_MODEL)),
        "w_in": nrm(ks[4], (DEPTH, D_MODEL, IN_DIM), D_MODEL ** -0.5),
        "conv_w": nrm(ks[5], (DEPTH, CONV_K, CONV_DIM), CONV_K ** -0.5),
        "w_conv_out": nrm(ks[6], (DEPTH, CONV_DIM, D_MODEL), CONV_DIM ** -0.5),
        "w_attn_out": nrm(ks[7], (DEPTH, ATTN_DIM, D_MODEL), ATTN_DIM ** -0.5),
        "w_mix_out": nrm(ks[8], (DEPTH, D_MODEL, D_MODEL), D_MODEL ** -0.5),
        "norm_xattn": gain(ks[9], (DEPTH, D_MODEL)),
        "norm_mem": gain(ks[10], (DEPTH, D_MODEL)),
        "w_xq": nrm(ks[11], (DEPTH, D_MODEL, X_DIM), D_MODEL ** -0.5),
        "w_xkv": nrm(ks[12], (DEPTH, D_MODEL, 2 * X_DIM), D_MODEL ** -0.5),
        "w_xo": nrm(ks[13], (DEPTH, X_DIM, D_MODEL), X_DIM ** -0.5),
        "norm_mlp": gain(ks[14], (DEPTH, D_MODEL)),
        "w_up": nrm(ks[15], (DEPTH, D_MODEL, D_FF), D_MODEL ** -0.5),
        "w_down": nrm(ks[16], (DEPTH, D_FF, D_MODEL), D_FF ** -0.5),
        "norm_final": gain(ks[17], (D_MODEL,)),
    }


def reference(x, mem, rel_bias, norm_mix, w_in, conv_w, w_conv_out, w_attn_out, w_mix_out,
              norm_xattn, norm_mem, w_xq, w_xkv, w_xo, norm_mlp, w_up, w_down, norm_final):
    B, S = x.shape[0], x.shape[1]
    for l in range(DEPTH):
        h = rmsnorm(x, norm_mix[l])
        z = h @ w_in[l]
        u, cb, cc, q, k, v, iq, ik, iw, ga, gb = jnp.split(z, SPLIT_POINTS, axis=-1)
        y_a = (cb * causal_dwconv(cc * u, conv_w[l])) @ w_conv_out[l]
        o = dsa_attention(
            q.reshape(B, S, N_HEADS, HEAD_DIM),
            k.reshape(B, S, N_KV_HEADS, HEAD_DIM),
            v.reshape(B, S, N_KV_HEADS, HEAD_DIM),
            iq.reshape(B, S, IDX_HEADS, IDX_DIM),
            ik,
            iw * (IDX_HEADS ** -0.5),
            rel_bias)
        y_b = o @ w_attn_out[l]
        merged = jax.nn.sigmoid(ga) * y_a + jax.nn.sigmoid(gb) * y_b
        x = x + merged @ w_mix_out[l]
        x = x + memory_cross_attention(rmsnorm(x, norm_xattn[l]), rmsnorm(mem, norm_mem[l]),
                                       w_xq[l], w_xkv[l], w_xo[l])
        h = rmsnorm(x, norm_mlp[l])
        x = x + jnp.square(jax.nn.relu(h @ w_up[l])) @ w_down[l]
    return rmsnorm(x, norm_final)
```

```python
import math
import numpy as np
import concourse.bass as bass
import concourse.mybir as mybir
from concourse.bass_utils import run_bass_kernel_spmd

F32 = mybir.dt.float32
BF16 = mybir.dt.bfloat16
AF = mybir.ActivationFunctionType
ALU = mybir.AluOpType
AX = mybir.AxisListType

D = 4096
S = 8192
NCORE = 8
NOWN = 1024
NH = 1040
CONV = 2048
IN_DIM = 19552
OFF_U, OFF_CB, OFF_CC, OFF_Q, OFF_K, OFF_V = 0, 2048, 4096, 6144, 8192, 8704
OFF_IQ, OFF_IK, OFF_IW, OFF_GA, OFF_GB = 9216, 11264, 11328, 11360, 15456
DFF = 16384
EPS = 1e-6
TOPK = 256
NEG = -30000.0
BIG = -1.0e30
ATT_SCALE = 128 ** -0.5
IW_SCALE = (32 ** -0.5) * (64 ** -0.5)
NBIS = 20


class Buf:
    __slots__ = ("name", "w", "r")

    def __init__(self, name):
        self.name = name
        self.w = None
        self.r = {}


class KB:
    COMPUTE = ("pe", "act", "dve", "pool")
    NPOOL = 12
    LIMIT = 30000

    def __init__(self, nc):
        self.nc = nc
        self.eng = {"pe": nc.tensor, "act": nc.scalar, "dve": nc.vector, "pool": nc.gpsimd, "sp": nc.sync}
        self.nsem = 0
        self.sem = {e: self._newsem(e) for e in self.COMPUTE}
        self.cnt = {e: 0 for e in self.COMPUTE}
        self.waited = {e: {} for e in self.eng}
        self.dsem = {q: [self._newsem("d" + q) for _ in range(self.NPOOL)] for q in ("sp", "pool")}
        self.dcnt = {q: [0] * self.NPOOL for q in ("sp", "pool")}
        self.drr = {q: 0 for q in ("sp", "pool")}
        self.nins = 0

    def _newsem(self, tag):
        self.nsem += 1
        return self.nc.alloc_semaphore(f"sm_{tag}_{self.nsem}")

    def _wait(self, e, tick):
        te, sem, val = tick
        key = id(sem)
        if self.waited[e].get(key, 0) >= val:
            return
        self.eng[e].wait_ge(sem, val)
        self.waited[e][key] = val

    def _deps(self, e, reads, writes):
        for b in reads:
            if b.w is not None:
                t = b.w
                if not (t[0] == e and e == "pe"):
                    self._wait(e, t)
        for b in writes:
            if b.w is not None:
                t = b.w
                if not (t[0] == e and e == "pe"):
                    self._wait(e, t)
            for t in b.r.values():
                if t[0] == e:
                    continue
                self._wait(e, t)

    def _mark(self, tick, reads, writes):
        for b in writes:
            b.w = tick
            b.r = {}
        for b in reads:
            b.r[(tick[0], id(tick[1]))] = tick

    def op(self, e, fn, reads=(), writes=(), inc=True):
        self._deps(e, reads, writes)
        ins = fn()
        self.nins += 1
        if inc:
            if self.cnt[e] >= self.LIMIT:
                self.sem[e] = self._newsem(e)
                self.cnt[e] = 0
            self.cnt[e] += 1
            ins.then_inc(self.sem[e], 1)
            tick = (e, self.sem[e], self.cnt[e])
        else:
            tick = (e, self.sem[e], self.cnt[e] + 1)
        self._mark(tick, reads, writes)
        return tick

    def dma(self, q, out, in_, reads=(), writes=()):
        if isinstance(in_, LazyIn):
            in_ = in_.ap()
        self._deps(q, reads, writes)
        i = self.drr[q] % self.NPOOL
        self.drr[q] += 1
        sem = self.dsem[q][i]
        if self.dcnt[q][i] > 0:
            self._wait(q, ("dma", sem, 16 * self.dcnt[q][i]))
        self.eng[q].dma_start(out=out, in_=in_).then_inc(sem, 16)
        self.nins += 1
        self.dcnt[q][i] += 1
        tick = ("dma", sem, 16 * self.dcnt[q][i])
        self._mark(tick, reads, writes)
        return tick

    def all_ticks(self):
        ts = []
        for e in self.COMPUTE:
            if self.cnt[e] > 0:
                ts.append((e, self.sem[e], self.cnt[e]))
        for q in self.dsem:
            for i, sem in enumerate(self.dsem[q]):
                if self.dcnt[q][i] > 0:
                    ts.append(("dma", sem, 16 * self.dcnt[q][i]))
        return ts

    def barrier(self, engines=None):
        ts = self.all_ticks()
        for e in (engines or self.eng):
            for t in ts:
                self._wait(e, t)


class LazyIn:
    def __init__(self, nc, name, shape, reg):
        self.nc, self.name, self.shape, self.reg, self._ap = nc, name, shape, reg, None

    def ap(self):
        if self._ap is None:
            self._ap = self.nc.dram_tensor(self.name, list(self.shape), F32, kind="ExternalInput").ap()
            self.reg.append(self.name)
        return self._ap

    def __getitem__(self, idx):
        return self.ap()[idx]

    def rearrange(self, *a, **kw):
        return self.ap().rearrange(*a, **kw)


class Rot:
    def __init__(self, items):
        self.items = items
        self.i = 0

    def next(self):
        it = self.items[self.i % len(self.items)]
        self.i += 1
        return it


def build_program(PHASES="KPACDEFG", DEBUG_OUT=()):
    nc = bass.Bass("TRN2", target_bir_lowering=False)
    k = KB(nc)

    in_names = []
    k.in_names = in_names

    def din(name, shape, dt=F32):
        return LazyIn(nc, name, shape, in_names)

    def dscr(name, shape, dt):
        if name in DEBUG_OUT:
            return nc.dram_tensor(name, list(shape), dt, kind="ExternalOutput").ap()
        return nc.dram_tensor(name, list(shape), dt).ap()

    xo = din("xo", [D, NH])
    xall = din("xall", [D, S])
    padmask_d = din("padmask", [128, 1024])
    w_in = din("w_in", [D, IN_DIM])
    convw_d = din("convw", [128, 16 * 3])
    w_ca = din("w_ca", [CONV, D])
    w_ba = din("w_ba", [2048, D])
    w_mo = din("w_mo", [D, D])
    w_xq = din("w_xq", [D, 512])
    w_xkv = din("w_xkv", [D, 1024])
    w_xo = din("w_xo", [512, D])
    w_up = din("w_up", [D, DFF])
    w_dn = din("w_dn", [DFF, D])
    gains_d = din("gains", [128, 5 * 32])
    memT = din("memT", [D, 256])
    biasn_d = din("biasn", [128, 2 * 16 * 128])
    bfar_d = din("bfar", [128, 16])
    identrep_d = din("identrep", [128, 512])
    cmask_d = din("cmask", [128, 128])
    xout = nc.dram_tensor("xout", [D, NOWN], F32, kind="ExternalOutput").ap()
    yn = nc.dram_tensor("yn", [D, NOWN], F32, kind="ExternalOutput").ap()

    KT_d = dscr("KT_d", [4, 128, S], BF16)
    V_d = dscr("V_d", [S, 512], BF16)
    IK_d = dscr("IK_d", [64, S], BF16)
    QT_d = dscr("QT_d", [2048, NOWN], BF16)
    IQT_d = dscr("IQT_d", [2048, NOWN], BF16)
    YAIN_d = dscr("YAIN_d", [2048, NOWN], BF16)
    OT_d = dscr("OT_d", [2048, NOWN], BF16)
    SGA_d = dscr("SGA_d", [D, NOWN], F32)
    SGB_d = dscr("SGB_d", [D, NOWN], F32)
    M_d = dscr("M_d", [D, NOWN], BF16)
    X1_d = dscr("X1_d", [D, NOWN], F32)
    X2_d = dscr("X2_d", [D, NOWN], F32)
    A_d = dscr("A_d", [DFF, NOWN], BF16)
    P_d = dscr("P_d", [D, NOWN], F32)
    b_KT, b_V, b_IK = Buf("KT_d"), Buf("V_d"), Buf("IK_d")
    b_QT, b_IQT, b_YAIN, b_OT = Buf("QT_d"), Buf("IQT_d"), Buf("YAIN_d"), Buf("OT_d")
    b_SGA, b_SGB, b_M, b_X1, b_X2, b_A, b_P = (Buf(n) for n in ("SGA", "SGB", "M", "X1", "X2", "A", "P"))
    b_xout = Buf("xout")
    b_yn = Buf("yn")

    psum = nc.alloc_psum_tensor("psum", [128, 8 * 512], F32).ap()

    def bank(i, n=512):
        return psum[:, i * 512:i * 512 + n]

    b_ps = [Buf(f"ps{i}") for i in range(8)]

    def sb(name, shape, dt):
        return nc.alloc_sbuf_tensor("sb_" + name, list(shape), dt).ap()

    gains = sb("gains", [128, 5 * 32], F32)
    convw = sb("convw", [128, 48], F32)
    ones_f = sb("ones_f", [128, 128], F32)
    ones_b = sb("ones_b", [128, 128], BF16)
    iw_sb = sb("iw_sb", [128, 8 * 32], F32)
    b_const = Buf("const")
    b_iw = Buf("iw")
    k.dma("sp", gains, gains_d, writes=[b_const])
    k.dma("sp", convw, convw_d, writes=[b_const])
    k.op("dve", lambda: nc.vector.memset(ones_f, 1.0), writes=[b_const])
    k.op("dve", lambda: nc.vector.memset(ones_b, 1.0), writes=[b_const])

    from contextlib import ExitStack

    uid = [0]

    def A(es, name, shape, dt):
        uid[0] += 1
        t = es.enter_context(nc.sbuf_tensor(f"t{uid[0]}_" + name, list(shape), dt))
        return t.ap() if hasattr(t, "ap") else t[:]

    def mkrot(es, name, n, shape, dt):
        return Rot([(A(es, f"{name}{i}", shape, dt), Buf(f"{name}{i}")) for i in range(n)])

    flip = [0]

    def alt():
        flip[0] ^= 1
        return "act" if flip[0] else "dve"

    def copy_evac(e, out, in_, reads, writes):
        if e == "act":
            return k.op("act", lambda: nc.scalar.copy(out=out, in_=in_), reads=reads, writes=writes)
        return k.op("dve", lambda: nc.vector.tensor_copy(out=out, in_=in_), reads=reads, writes=writes)

    def rmsnorm_T(src, ncols, gidx, res, ps_banks, dst3=None, dst_buf=None, src_buf=None, out_d=None, out_buf=None):
        ntiles = [(o, min(512, ncols - o)) for o in range(0, ncols, 512)]
        rd = [src_buf] if src_buf is not None else []
        rstd, b_rstd = res["rstd"]
        for c in range(32):
            xc, bx = res["xc"].next()
            k.dma("sp", xc[:, :ncols], src[c * 128:(c + 1) * 128, :], reads=rd, writes=[bx])
            sq, bq = res["sq"].next()
            k.op("act", lambda xc=xc, sq=sq: nc.scalar.activation(out=sq[:, :ncols], in_=xc[:, :ncols], func=AF.Square),
                 reads=[bx], writes=[bq])
            for ti, (o, n) in enumerate(ntiles):
                pb = ps_banks[ti]
                k.op("pe", lambda pb=pb, sq=sq, o=o, n=n, c=c: nc.tensor.matmul(
                    bank(pb, n), ones_f, sq[:, o:o + n], start=(c == 0), stop=(c == 31)),
                    reads=[bq, b_const], writes=[b_ps[pb]], inc=(c == 31 or ti == len(ntiles) - 1))
        for ti, (o, n) in enumerate(ntiles):
            pb = ps_banks[ti]
            k.op("dve", lambda pb=pb, o=o, n=n: nc.vector.tensor_scalar(
                out=rstd[:, o:o + n], in0=bank(pb, n), scalar1=1.0 / D, scalar2=EPS, op0=ALU.mult, op1=ALU.add),
                reads=[b_ps[pb]], writes=[b_rstd])
        k.op("act", lambda: nc.scalar.activation(out=rstd[:, :ncols], in_=rstd[:, :ncols], func=AF.Sqrt),
             reads=[b_rstd], writes=[b_rstd])
        k.op("dve", lambda: nc.vector.reciprocal(out=rstd[:, :ncols], in_=rstd[:, :ncols]),
             reads=[b_rstd], writes=[b_rstd])
        for c in range(32):
            xc, bx = res["xc"].next()
            k.dma("sp", xc[:, :ncols], src[c * 128:(c + 1) * 128, :], reads=rd, writes=[bx])
            gs = gains[:, gidx * 32 + c:gidx * 32 + c + 1]
            if out_d is None:
                k.op("dve", lambda xc=xc, c=c, gs=gs: nc.vector.scalar_tensor_tensor(
                    out=dst3[:, c, :ncols], in0=xc[:, :ncols], scalar=gs,
                    in1=rstd[:, :ncols], op0=ALU.mult, op1=ALU.mult),
                    reads=[bx, b_rstd, b_const], writes=[dst_buf])
            else:
                of, bo = res["of"].next()
                k.op("dve", lambda xc=xc, of=of, gs=gs: nc.vector.scalar_tensor_tensor(
                    out=of[:, :ncols], in0=xc[:, :ncols], scalar=gs,
                    in1=rstd[:, :ncols], op0=ALU.mult, op1=ALU.mult),
                    reads=[bx, b_rstd, b_const], writes=[bo])
                k.dma("sp", out_d[c * 128:(c + 1) * 128, :], of[:, :ncols], reads=[bo], writes=[out_buf])

    def wview(wd, r0, nrows, c0, ncols):
        return wd[r0:r0 + nrows, c0:c0 + ncols].rearrange("(c p) n -> p c n", p=128)

    def proj(w3, bw, nk, col0, rhs_fn, rhs_bufs, ntiles, pbanks, m=128):
        for kc in range(nk):
            for ti, (o, n) in enumerate(ntiles):
                pb = pbanks[ti]
                k.op("pe", lambda pb=pb, kc=kc, o=o, n=n: nc.tensor.matmul(
                    psum[0:m, pb * 512:pb * 512 + n], w3[:, kc, col0:col0 + m], rhs_fn(kc, o, n),
                    start=(kc == 0), stop=(kc == nk - 1)),
                    reads=[bw] + rhs_bufs, writes=[b_ps[pb]], inc=(kc == nk - 1))

    NT_OWN = [(0, 512), (512, 512)]
    NT_HALO = [(0, 512), (512, 512), (1024, 16)]

    def pair(pb):
        return psum[:, pb * 512:pb * 512 + 1024]

    def phase_K():
        with ExitStack() as es:
            wk = A(es, "wk", [128, 32, 512], BF16)
            wv = A(es, "wv", [128, 32, 512], BF16)
            wik = A(es, "wik", [128, 32, 64], BF16)
            bwk, bwv, bwik = Buf("wk"), Buf("wv"), Buf("wik")
            for c0 in (0, 256):
                k.dma("pool", wk[:, :, c0:c0 + 256], wview(w_in, 0, D, OFF_K + c0, 256), writes=[bwk])
            for c0 in (0, 256):
                k.dma("pool", wv[:, :, c0:c0 + 256], wview(w_in, 0, D, OFF_V + c0, 256), writes=[bwv])
            k.dma("pool", wik, wview(w_in, 0, D, OFF_IK, 64), writes=[bwik])
            hts = mkrot(es, "htk", 2, [128, 32, 512], BF16)
            res = dict(xc=mkrot(es, "xck", 3, [128, 512], F32), sq=mkrot(es, "sqk", 2, [128, 512], F32),
                       rstd=(A(es, "rstdk", [128, 512], F32), Buf("rstdk")))
            kto = mkrot(es, "kto", 2, [128, 4, 512], BF16)
            iko = mkrot(es, "iko", 2, [64, 512], BF16)
            vo = mkrot(es, "vo", 2, [128, 4, 512], BF16)
            KT_v = KT_d.rearrange("g p n -> p g n")
            V_v = V_d.rearrange("(b p) n -> p b n", p=128)
            prot = Rot([0, 1, 2, 3, 4, 5])
            for tt in range(S // 512):
                h3, bh = hts.next()
                rmsnorm_T(xall[:, tt * 512:(tt + 1) * 512], 512, 0, res, (7,), dst3=h3, dst_buf=bh)
                kt, bkt = kto.next()
                for g in range(4):
                    pb = prot.next()
                    proj(wk, bwk, 32, g * 128, lambda kc, o, n, h3=h3: h3[:, kc, o:o + n], [bh], [(0, 512)], (pb,))
                    copy_evac(alt(), kt[:, g, :], bank(pb), [b_ps[pb]], [bkt])
                k.dma("sp", KT_v[:, :, tt * 512:(tt + 1) * 512], kt, reads=[bkt], writes=[b_KT])
                ikt, bikt = iko.next()
                pb = prot.next()
                proj(wik, bwik, 32, 0, lambda kc, o, n, h3=h3: h3[:, kc, o:o + n], [bh], [(0, 512)], (pb,), m=64)
                copy_evac(alt(), ikt, psum[0:64, pb * 512:(pb + 1) * 512], [b_ps[pb]], [bikt])
                k.dma("sp", IK_d[:, tt * 512:(tt + 1) * 512], ikt, reads=[bikt], writes=[b_IK])
                vt, bvt = vo.next()
                for s4 in range(4):
                    pb = prot.next()
                    for kc in range(32):
                        k.op("pe", lambda pb=pb, kc=kc, s4=s4, h3=h3: nc.tensor.matmul(
                            bank(pb), h3[:, kc, s4 * 128:(s4 + 1) * 128], wv[:, kc, :], start=(kc == 0), stop=(kc == 31)),
                            reads=[bwv, bh], writes=[b_ps[pb]], inc=(kc == 31))
                    copy_evac(alt(), vt[:, s4, :], bank(pb), [b_ps[pb]], [bvt])
                k.dma("sp", V_v[:, tt * 4:(tt + 1) * 4, :], vt, reads=[bvt], writes=[b_V])
            k.barrier()

    def phase_P():
        with ExitStack() as es:
            h3 = A(es, "hTo", [128, 32, NH], BF16)
            bh = Buf("hTo")
            with ExitStack() as es2:
                res = dict(xc=mkrot(es2, "xcp", 3, [128, NH], F32), sq=mkrot(es2, "sqp", 2, [128, NH], F32),
                           rstd=(A(es2, "rstdp", [128, NH], F32), Buf("rstdp")))
                rmsnorm_T(xo, NH, 0, res, (5, 6, 7), dst3=h3, dst_buf=bh)
                k.barrier()
            wrot = mkrot(es, "wp", 2, [128, 32, 384], BF16)
            rhs_h = lambda kc, o, n: h3[:, kc, o:o + n]
            uf = A(es, "uf", [128, NH], F32)
            cu = A(es, "cu", [128, 8, 130], F32)
            cv = A(es, "cv", [128, 8, 128], F32)
            b_uf, b_cu, b_cv = Buf("uf"), Buf("cu"), Buf("cv")
            yo = mkrot(es, "yo", 2, [128, 1024], BF16)
            for i in range(16):
                w3, bw = wrot.next()
                for si, off in enumerate((OFF_U, OFF_CB, OFF_CC)):
                    k.dma("pool", w3[:, :, si * 128:(si + 1) * 128], wview(w_in, 0, D, off + i * 128, 128), writes=[bw])
                proj(w3, bw, 32, 0, rhs_h, [bh], NT_HALO, (0, 1, 4))
                k.op("act", lambda: nc.scalar.copy(out=uf[:, 0:1024], in_=pair(0)), reads=[b_ps[0], b_ps[1]], writes=[b_uf])
                k.op("act", lambda: nc.scalar.copy(out=uf[:, 1024:1040], in_=bank(4, 16)), reads=[b_ps[4]], writes=[b_uf])
                proj(w3, bw, 32, 256, rhs_h, [bh], NT_HALO, (2, 3, 5))
                k.op("dve", lambda: nc.vector.tensor_tensor(
                    out=cu[:, :, 2:130], in0=pair(2).rearrange("p (j t) -> p j t", j=8),
                    in1=uf[:, 0:1024].rearrange("p (j t) -> p j t", j=8), op=ALU.mult),
                    reads=[b_ps[2], b_ps[3], b_uf], writes=[b_cu])
                k.op("dve", lambda: nc.vector.tensor_tensor(
                    out=cu[:, :, 0:2], in0=bank(5, 16).rearrange("p (j t) -> p j t", j=8),
                    in1=uf[:, 1024:1040].rearrange("p (j t) -> p j t", j=8), op=ALU.mult),
                    reads=[b_ps[5], b_uf], writes=[b_cu])
                k.op("dve", lambda i=i: nc.vector.tensor_scalar(
                    out=cv, in0=cu[:, :, 2:130], scalar1=convw[:, i * 3 + 2:i * 3 + 3], scalar2=None, op0=ALU.mult),
                    reads=[b_cu, b_const], writes=[b_cv])
                k.op("dve", lambda i=i: nc.vector.scalar_tensor_tensor(
                    out=cv, in0=cu[:, :, 1:129], scalar=convw[:, i * 3 + 1:i * 3 + 2], in1=cv, op0=ALU.mult, op1=ALU.add),
                    reads=[b_cu, b_cv, b_const], writes=[b_cv])
                k.op("dve", lambda i=i: nc.vector.scalar_tensor_tensor(
                    out=cv, in0=cu[:, :, 0:128], scalar=convw[:, i * 3:i * 3 + 1], in1=cv, op0=ALU.mult, op1=ALU.add),
                    reads=[b_cu, b_cv, b_const], writes=[b_cv])
                proj(w3, bw, 32, 128, rhs_h, [bh], NT_OWN, (6, 7))
                y, by = yo.next()
                k.op("dve", lambda y=y: nc.vector.tensor_tensor(
                    out=y, in0=pair(6), in1=cv.rearrange("p j t -> p (j t)"), op=ALU.mult),
                    reads=[b_ps[6], b_ps[7], b_cv], writes=[by])
                k.dma("sp", YAIN_d[i * 128:(i + 1) * 128, :], y, reads=[by], writes=[b_YAIN])
            ob = mkrot(es, "obp", 3, [128, 1024], BF16)
            prs = Rot([0, 2, 4, 6])
            for off, dst, bdst in ((OFF_Q, QT_d, b_QT), (OFF_IQ, IQT_d, b_IQT)):
                for t in range(8):
                    w3, bw = wrot.next()
                    k.dma("pool", w3[:, :, 0:256], wview(w_in, 0, D, off + t * 256, 256), writes=[bw])
                    for cch in range(2):
                        pb = prs.next()
                        proj(w3, bw, 32, cch * 128, rhs_h, [bh], NT_OWN, (pb, pb + 1))
                        o_, bo = ob.next()
                        copy_evac(alt(), o_, pair(pb), [b_ps[pb], b_ps[pb + 1]], [bo])
                        r0 = (t * 2 + cch) * 128
                        k.dma("sp", dst[r0:r0 + 128, :], o_, reads=[bo], writes=[bdst])
            w3, bw = wrot.next()
            k.dma("pool", w3[:, :, 0:32], wview(w_in, 0, D, OFF_IW, 32), writes=[bw])
            pb = prs.next()
            for j in range(8):
                for kc in range(32):
                    k.op("pe", lambda j=j, kc=kc, pb=pb, w3=w3: nc.tensor.matmul(
                        psum[:, pb * 512 + j * 32:pb * 512 + (j + 1) * 32], h3[:, kc, j * 128:(j + 1) * 128],
                        w3[:, kc, 0:32], start=(kc == 0), stop=(kc == 31)),
                        reads=[bw, bh], writes=[b_ps[pb]], inc=(kc == 31))
            k.op("dve", lambda pb=pb: nc.vector.tensor_scalar(
                out=iw_sb, in0=bank(pb, 256), scalar1=IW_SCALE, scalar2=None, op0=ALU.mult),
                reads=[b_ps[pb]], writes=[b_iw])
            og = mkrot(es, "ogp", 3, [128, 1024], F32)
            for off, dst, bdst in ((OFF_GA, SGA_d, b_SGA), (OFF_GB, SGB_d, b_SGB)):
                for t in range(16):
                    w3, bw = wrot.next()
                    k.dma("pool", w3[:, :, 0:256], wview(w_in, 0, D, off + t * 256, 256), writes=[bw])
                    for cch in range(2):
                        pb = prs.next()
                        proj(w3, bw, 32, cch * 128, rhs_h, [bh], NT_OWN, (pb, pb + 1))
                        o_, bo = og.next()
                        k.op("act", lambda o_=o_, pb=pb: nc.scalar.activation(out=o_, in_=pair(pb), func=AF.Sigmoid),
                             reads=[b_ps[pb], b_ps[pb + 1]], writes=[bo])
                        r0 = (t * 2 + cch) * 128
                        k.dma("sp", dst[r0:r0 + 128, :], o_, reads=[bo], writes=[bdst])
            k.barrier()

    def phase_A():
        with ExitStack() as es:
            identrep = A(es, "identrep", [128, 512], BF16)
            cmask = A(es, "cmask", [128, 128], F32)
            padmask = A(es, "padmask", [128, 1024], F32)
            biasn = A(es, "biasn", [128, 2 * 16 * 128], F32)
            bfar = A(es, "bfar", [128, 16], F32)
            b_ac = Buf("attnconst")
            k.dma("sp", cmask, cmask_d, writes=[b_ac])
            k.dma("sp", padmask, padmask_d, writes=[b_ac])
            k.dma("sp", biasn, biasn_d, writes=[b_ac])
            k.dma("sp", bfar, bfar_d, writes=[b_ac])
            k.dma("pool", identrep, identrep_d, writes=[b_ac])
            for off in range(2):
                for h in range(16):
                    sl = biasn[:, (off * 16 + h) * 128:(off * 16 + h + 1) * 128]
                    k.op("dve", lambda sl=sl, h=h: nc.vector.tensor_scalar(
                        out=sl, in0=sl, scalar1=bfar[:, h:h + 1], scalar2=None, op0=ALU.subtract),
                        reads=[b_ac], writes=[b_ac])
            sc = A(es, "sc", [128, S], F32)
            b_sc = [Buf(f"sc{i}") for i in range(16)]
            negm = A(es, "negm", [128, S], BF16)
            b_negm = Buf("negm")
            qTs = mkrot(es, "qT", 2, [128, 2048], BF16)
            iqTs = mkrot(es, "iqT", 2, [128, 16, 128], BF16)
            ik2 = mkrot(es, "ik2", 3, [128, 512], BF16)
            rr = mkrot(es, "rr", 3, [128, 512], BF16)
            sm = A(es, "sm", [128, 16], F32)
            b_sm = Buf("sm")
            mx, mn, lo, hi, mid, cnt, ge, d1, half = (sm[:, i:i + 1] for i in range(9))
            k.op("dve", lambda: nc.vector.memset(half, 0.5), writes=[b_sm])
            ktc = mkrot(es, "ktc", 2, [128, 1024], BF16)
            vc = mkrot(es, "vc", 2, [128, 8, 128], BF16)
            pp = mkrot(es, "pp", 3, [128, 512], BF16)
            tmpf = mkrot(es, "tmpf", 2, [128, 512], F32)
            rs = A(es, "rs", [128, 512], F32)
            b_rs = Buf("rs")
            oo = mkrot(es, "oo", 2, [128, 512], BF16)
            QT_v = QT_d.rearrange("(h p) n -> p h n", p=128)
            IQT_v = IQT_d.rearrange("(h p) n -> p h n", p=128)
            OT_v = OT_d.rearrange("(h p) n -> p h n", p=128)
            V_v = V_d.rearrange("(b p) n -> p b n", p=128)
            irot = Rot([0, 1, 2, 3])
            lrot = Rot([6, 7])
            for j in range(8):
                L = 1024 * (j + 1)
                nkb = 8 * (j + 1)
                qT, bq = qTs.next()
                iqT, biq = iqTs.next()
                k.dma("sp", qT.rearrange("p (h t) -> p h t", h=16), QT_v[:, :, j * 128:(j + 1) * 128], reads=[b_QT], writes=[bq])
                k.dma("sp", iqT, IQT_v[:, :, j * 128:(j + 1) * 128], reads=[b_IQT], writes=[biq])
                for c5 in range(2 * (j + 1)):
                    ikt, bik = ik2.next()
                    k.dma("sp", ikt[0:64, :], IK_d[:, c5 * 512:(c5 + 1) * 512], reads=[b_IK], writes=[bik])
                    k.dma("sp", ikt[64:128, :], IK_d[:, c5 * 512:(c5 + 1) * 512], reads=[b_IK], writes=[bik])
                    scs = sc[:, c5 * 512:(c5 + 1) * 512]
                    for h in range(32):
                        ch, hf = h // 2, h % 2
                        pb = irot.next()
                        k.op("pe", lambda pb=pb, ch=ch, hf=hf, iqT=iqT, ikt=ikt: nc.tensor.matmul(
                            bank(pb), iqT[hf * 64:(hf + 1) * 64, ch, :], ikt[hf * 64:(hf + 1) * 64, :], start=True, stop=True),
                            reads=[biq, bik], writes=[b_ps[pb]])
                        r_, br = rr.next()
                        k.op("act", lambda pb=pb, r_=r_: nc.scalar.activation(out=r_, in_=bank(pb), func=AF.Relu),
                             reads=[b_ps[pb]], writes=[br])
                        iws = iw_sb[:, j * 32 + h:j * 32 + h + 1]
                        if h == 0:
                            k.op("dve", lambda r_=r_, iws=iws, scs=scs: nc.vector.tensor_scalar(
                                out=scs, in0=r_, scalar1=iws, scalar2=None, op0=ALU.mult),
                                reads=[br, b_iw], writes=[b_sc[c5]])
                        else:
                            k.op("dve", lambda r_=r_, iws=iws, scs=scs: nc.vector.scalar_tensor_tensor(
                                out=scs, in0=r_, scalar=iws, in1=scs, op0=ALU.mult, op1=ALU.add),
                                reads=[br, b_iw, b_sc[c5]], writes=[b_sc[c5]])
                scb = b_sc[:2 * (j + 1)]
                scL = sc[:, :L]
                k.op("dve", lambda scL=scL: nc.vector.tensor_reduce(out=mx, in_=scL, axis=AX.X, op=ALU.max), reads=scb, writes=[b_sm])
                k.op("dve", lambda scL=scL: nc.vector.tensor_reduce(out=mn, in_=scL, axis=AX.X, op=ALU.min), reads=scb, writes=[b_sm])
                k.op("dve", lambda: nc.vector.tensor_scalar(out=lo, in0=mn, scalar1=-1.0, scalar2=None, op0=ALU.add), reads=[b_sm], writes=[b_sm])
                k.op("dve", lambda: nc.vector.tensor_copy(out=hi, in_=mx), reads=[b_sm], writes=[b_sm])
                k.op("dve", lambda: nc.vector.tensor_tensor(out=sc[:, 0:1024], in0=sc[:, 0:1024], in1=padmask, op=ALU.add),
                     reads=[b_sc[0], b_sc[1], b_ac], writes=[b_sc[0], b_sc[1]])
                k.op("dve", lambda L=L: nc.vector.tensor_tensor(out=sc[:, L - 128:L], in0=sc[:, L - 128:L], in1=cmask, op=ALU.add),
                     reads=[b_sc[2 * j + 1], b_ac], writes=[b_sc[2 * j + 1]])
                for it in range(NBIS):
                    k.op("dve", lambda: nc.vector.scalar_tensor_tensor(out=mid, in0=lo, scalar=hi, in1=half, op0=ALU.add, op1=ALU.mult),
                         reads=[b_sm], writes=[b_sm])
                    k.op("dve", lambda scL=scL, L=L: nc.vector.tensor_scalar(
                        out=negm[:, :L], in0=scL, scalar1=mid, scalar2=0.0, op0=ALU.is_ge, op1=ALU.add, accum_out=cnt),
                        reads=scb + [b_sm], writes=[b_negm, b_sm])
                    k.op("dve", lambda: nc.vector.tensor_scalar(out=ge, in0=cnt, scalar1=TOPK - 0.5, scalar2=None, op0=ALU.is_ge),
                         reads=[b_sm], writes=[b_sm])
                    k.op("dve", lambda: nc.vector.tensor_tensor(out=d1, in0=mid, in1=lo, op=ALU.subtract), reads=[b_sm], writes=[b_sm])
                    k.op("dve", lambda: nc.vector.scalar_tensor_tensor(out=lo, in0=d1, scalar=ge, in1=lo, op0=ALU.mult, op1=ALU.add),
                         reads=[b_sm], writes=[b_sm])
                    k.op("dve", lambda: nc.vector.tensor_tensor(out=d1, in0=hi, in1=mid, op=ALU.subtract), reads=[b_sm], writes=[b_sm])
                    k.op("dve", lambda: nc.vector.scalar_tensor_tensor(out=hi, in0=d1, scalar=ge, in1=mid, op0=ALU.mult, op1=ALU.add),
                         reads=[b_sm], writes=[b_sm])
                k.op("dve", lambda scL=scL, L=L: nc.vector.tensor_scalar(
                    out=negm[:, :L], in0=scL, scalar1=lo, scalar2=NEG, op0=ALU.is_lt, op1=ALU.mult),
                    reads=scb + [b_sm], writes=[b_negm])
                for g in range(4):
                    qg = qT[:, g * 512:(g + 1) * 512]
                    pend = None
                    for chn in range(j + 1):
                        kt, bkt = ktc.next()
                        vt, bvt = vc.next()
                        k.dma("sp", kt, KT_d[g, :, chn * 1024:(chn + 1) * 1024], reads=[b_KT], writes=[bkt])
                        k.dma("sp", vt, V_v[:, chn * 8:(chn + 1) * 8, g * 128:(g + 1) * 128], reads=[b_V], writes=[bvt])
                        for b in range(8):
                            kb = chn * 8 + b
                            pl = lrot.next()
                            k.op("pe", lambda pl=pl, kt=kt, b=b, qg=qg: nc.tensor.matmul(
                                bank(pl), kt[:, b * 128:(b + 1) * 128], qg, start=True, stop=False),
                                reads=[bkt, bq], writes=[b_ps[pl]], inc=False)
                            k.op("pe", lambda pl=pl, kb=kb: nc.tensor.matmul(
                                bank(pl), negm[:, kb * 128:(kb + 1) * 128], identrep, start=False, stop=True),
                                reads=[b_negm, b_ac], writes=[b_ps[pl]])
                            p_, bp = pp.next()
                            if kb >= nkb - 2:
                                off = 0 if kb == nkb - 1 else 1
                                tf, btf = tmpf.next()
                                bsl = biasn[:, (off * 16 + 4 * g) * 128:(off * 16 + 4 * g + 4) * 128]
                                k.op("dve", lambda pl=pl, tf=tf, bsl=bsl: nc.vector.scalar_tensor_tensor(
                                    out=tf, in0=bank(pl), scalar=ATT_SCALE, in1=bsl, op0=ALU.mult, op1=ALU.add),
                                    reads=[b_ps[pl], b_ac], writes=[btf])
                                k.op("act", lambda tf=tf, p_=p_: nc.scalar.activation(out=p_, in_=tf, func=AF.Exp),
                                     reads=[btf], writes=[bp])
                            else:
                                k.op("act", lambda pl=pl, p_=p_: nc.scalar.activation(out=p_, in_=bank(pl), func=AF.Exp, scale=ATT_SCALE),
                                     reads=[b_ps[pl]], writes=[bp])
                            if pend is not None:
                                pend()
                            def pv(vt=vt, b=b, p_=p_, bp=bp, bvt=bvt, kb=kb):
                                k.op("pe", lambda: nc.tensor.matmul(bank(4), vt[:, b, :], p_, start=(kb == 0), stop=(kb == nkb - 1)),
                                     reads=[bvt, bp], writes=[b_ps[4]], inc=False)
                                k.op("pe", lambda: nc.tensor.matmul(bank(5), ones_b, p_, start=(kb == 0), stop=(kb == nkb - 1)),
                                     reads=[b_const, bp], writes=[b_ps[5]])
                            pend = pv
                    pend()
                    k.op("dve", lambda: nc.vector.reciprocal(out=rs, in_=bank(5)), reads=[b_ps[5]], writes=[b_rs])
                    o_, bo = oo.next()
                    k.op("dve", lambda o_=o_: nc.vector.tensor_tensor(out=o_, in0=bank(4), in1=rs, op=ALU.mult),
                         reads=[b_ps[4], b_rs], writes=[bo])
                    k.dma("sp", OT_v[:, 4 * g:4 * g + 4, j * 128:(j + 1) * 128], o_.rearrange("p (h t) -> p h t", h=4),
                          reads=[bo], writes=[b_OT])
            k.barrier()

    def phase_C():
        with ExitStack() as es:
            yain = A(es, "yain", [128, 16, 1024], BF16)
            ot = A(es, "ot", [128, 16, 1024], BF16)
            b_yain, b_ot = Buf("yain"), Buf("ot")
            for c0 in range(0, 16, 4):
                k.dma("sp", yain[:, c0:c0 + 4, :], YAIN_d[c0 * 128:(c0 + 4) * 128, :].rearrange("(c p) n -> p c n", p=128),
                      reads=[b_YAIN], writes=[b_yain])
                k.dma("sp", ot[:, c0:c0 + 4, :], OT_d[c0 * 128:(c0 + 4) * 128, :].rearrange("(c p) n -> p c n", p=128),
                      reads=[b_OT], writes=[b_ot])
            wa = mkrot(es, "wca", 2, [128, 16, 256], BF16)
            wb = mkrot(es, "wba", 2, [128, 16, 256], BF16)
            sga = mkrot(es, "sga", 2, [128, 1024], F32)
            sgb = mkrot(es, "sgb", 2, [128, 1024], F32)
            mo = mkrot(es, "mo", 2, [128, 1024], BF16)
            prs = Rot([(0, 2), (4, 6)])
            for t in range(16):
                wa3, bwa = wa.next()
                wb3, bwb = wb.next()
                k.dma("pool", wa3, wview(w_ca, 0, 2048, t * 256, 256), writes=[bwa])
                k.dma("pool", wb3, wview(w_ba, 0, 2048, t * 256, 256), writes=[bwb])
                for cch in range(2):
                    oc = t * 2 + cch
                    pa, pb = prs.next()
                    sa, bsa = sga.next()
                    sb_, bsb = sgb.next()
                    k.dma("sp", sa, SGA_d[oc * 128:(oc + 1) * 128, :], reads=[b_SGA], writes=[bsa])
                    k.dma("sp", sb_, SGB_d[oc * 128:(oc + 1) * 128, :], reads=[b_SGB], writes=[bsb])
                    proj(wa3, bwa, 16, cch * 128, lambda kc, o, n: yain[:, kc, o:o + n], [b_yain], NT_OWN, (pa, pa + 1))
                    proj(wb3, bwb, 16, cch * 128, lambda kc, o, n: ot[:, kc, o:o + n], [b_ot], NT_OWN, (pb, pb + 1))
                    k.op("dve", lambda sa=sa, pa=pa: nc.vector.tensor_tensor(out=sa, in0=pair(pa), in1=sa, op=ALU.mult),
                         reads=[b_ps[pa], b_ps[pa + 1], bsa], writes=[bsa])
                    k.op("dve", lambda sb_=sb_, pb=pb: nc.vector.tensor_tensor(out=sb_, in0=pair(pb), in1=sb_, op=ALU.mult),
                         reads=[b_ps[pb], b_ps[pb + 1], bsb], writes=[bsb])
                    m_, bm = mo.next()
                    k.op("dve", lambda m_=m_, sa=sa, sb_=sb_: nc.vector.tensor_tensor(out=m_, in0=sa, in1=sb_, op=ALU.add),
                         reads=[bsa, bsb], writes=[bm])
                    k.dma("sp", M_d[oc * 128:(oc + 1) * 128, :], m_, reads=[bm], writes=[b_M])
            k.barrier()

    def resid_proj(es, src3, b_src, nk, wd, xin_fn, xin_buf, xout_d, b_xo, wshape_cols=256):
        wr = mkrot(es, "wrp", 2, [128, nk, wshape_cols], BF16)
        xr = mkrot(es, "xrp", 3, [128, 1024], F32)
        prs = Rot([0, 2, 4, 6])
        for t in range(D // wshape_cols):
            w3, bw = wr.next()
            k.dma("pool", w3, wview(wd, 0, nk * 128, t * wshape_cols, wshape_cols), writes=[bw])
            for cch in range(wshape_cols // 128):
                oc = t * (wshape_cols // 128) + cch
                pb = prs.next()
                x_, bx = xr.next()
                k.dma("sp", x_, xin_fn(oc), reads=[xin_buf] if xin_buf is not None else [], writes=[bx])
                proj(w3, bw, nk, cch * 128, lambda kc, o, n: src3[:, kc, o:o + n], [b_src], NT_OWN, (pb, pb + 1))
                k.op("dve", lambda x_=x_, pb=pb: nc.vector.tensor_tensor(out=x_, in0=pair(pb), in1=x_, op=ALU.add),
                     reads=[b_ps[pb], b_ps[pb + 1], bx], writes=[bx])
                k.dma("sp", xout_d[oc * 128:(oc + 1) * 128, :], x_, reads=[bx], writes=[b_xo])

    def phase_D():
        with ExitStack() as es:
            mg = A(es, "mg", [128, 32, 1024], BF16)
            b_mg = Buf("mg")
            for c0 in range(0, 32, 4):
                k.dma("sp", mg[:, c0:c0 + 4, :], M_d[c0 * 128:(c0 + 4) * 128, :].rearrange("(c p) n -> p c n", p=128),
                      reads=[b_M], writes=[b_mg])
            resid_proj(es, mg, b_mg, 32, w_mo, lambda oc: xo[oc * 128:(oc + 1) * 128, 0:1024], None, X1_d, b_X1)
            k.barrier()

    def phase_E():
        with ExitStack() as es:
            hx = A(es, "hx", [128, 32, 1024], BF16)
            hm = A(es, "hm", [128, 32, 256], BF16)
            b_hx, b_hm = Buf("hx"), Buf("hm")
            qx = A(es, "qx", [128, 4, 1024], BF16)
            kx = A(es, "kx", [128, 4, 256], BF16)
            vx = A(es, "vx", [128, 2, 512], BF16)
            ox = A(es, "ox", [128, 4, 1024], BF16)
            b_qx, b_kx, b_vx, b_ox = Buf("qx"), Buf("kx"), Buf("vx"), Buf("ox")
            with ExitStack() as es2:
                res = dict(xc=mkrot(es2, "xce", 3, [128, 1024], F32), sq=mkrot(es2, "sqe", 2, [128, 1024], F32),
                           rstd=(A(es2, "rstde", [128, 1024], F32), Buf("rstde")))
                rmsnorm_T(X1_d, 1024, 1, res, (5, 6), dst3=hx, dst_buf=b_hx, src_buf=b_X1)
                rmsnorm_T(memT, 256, 2, res, (7,), dst3=hm, dst_buf=b_hm)
                k.barrier()
            wr = mkrot(es, "wxe", 2, [128, 32, 256], BF16)
            prs = Rot([0, 2])
            for t in range(2):
                w3, bw = wr.next()
                k.dma("pool", w3, wview(w_xq, 0, D, t * 256, 256), writes=[bw])
                for cch in range(2):
                    pb = prs.next()
                    proj(w3, bw, 32, cch * 128, lambda kc, o, n: hx[:, kc, o:o + n], [b_hx], NT_OWN, (pb, pb + 1))
                    copy_evac(alt(), qx[:, t * 2 + cch, :], pair(pb), [b_ps[pb], b_ps[pb + 1]], [b_qx])
            for t in range(2):
                w3, bw = wr.next()
                k.dma("pool", w3, wview(w_xkv, 0, D, t * 256, 256), writes=[bw])
                for cch in range(2):
                    pb = prs.next()
                    proj(w3, bw, 32, cch * 128, lambda kc, o, n: hm[:, kc, o:o + n], [b_hm], [(0, 256)], (pb,))
                    copy_evac(alt(), kx[:, t * 2 + cch, :], bank(pb, 256), [b_ps[pb]], [b_kx])
            for t in range(2):
                w3, bw = wr.next()
                k.dma("pool", w3, wview(w_xkv, 0, D, 512 + t * 256, 256), writes=[bw])
                for mc in range(2):
                    pb = prs.next()
                    for kc in range(32):
                        k.op("pe", lambda pb=pb, kc=kc, mc=mc, w3=w3: nc.tensor.matmul(
                            bank(pb, 256), hm[:, kc, mc * 128:(mc + 1) * 128], w3[:, kc, :], start=(kc == 0), stop=(kc == 31)),
                            reads=[bw, b_hm], writes=[b_ps[pb]], inc=(kc == 31))
                    copy_evac(alt(), vx[:, mc, t * 256:(t + 1) * 256], bank(pb, 256), [b_ps[pb]], [b_vx])
            pp = mkrot(es, "ppe", 4, [128, 512], BF16)
            rs = A(es, "rse", [128, 512], F32)
            b_rs = Buf("rse")
            lrot = Rot([0, 1, 2, 3])
            for h in range(4):
                for tt in range(2):
                    ps_ = []
                    for mc in range(2):
                        pl = lrot.next()
                        k.op("pe", lambda pl=pl, h=h, mc=mc, tt=tt: nc.tensor.matmul(
                            bank(pl), kx[:, h, mc * 128:(mc + 1) * 128], qx[:, h, tt * 512:(tt + 1) * 512], start=True, stop=True),
                            reads=[b_kx, b_qx], writes=[b_ps[pl]])
                        p_, bp = pp.next()
                        k.op("act", lambda pl=pl, p_=p_: nc.scalar.activation(out=p_, in_=bank(pl), func=AF.Exp, scale=ATT_SCALE),
                             reads=[b_ps[pl]], writes=[bp])
                        ps_.append((p_, bp))
                    for mc in range(2):
                        p_, bp = ps_[mc]
                        k.op("pe", lambda p_=p_, mc=mc, h=h: nc.tensor.matmul(
                            bank(4), vx[:, mc, h * 128:(h + 1) * 128], p_, start=(mc == 0), stop=(mc == 1)),
                            reads=[b_vx, bp], writes=[b_ps[4]], inc=False)
                        k.op("pe", lambda p_=p_, mc=mc: nc.tensor.matmul(bank(5), ones_b, p_, start=(mc == 0), stop=(mc == 1)),
                             reads=[b_const, bp], writes=[b_ps[5]])
                    k.op("dve", lambda: nc.vector.reciprocal(out=rs, in_=bank(5)), reads=[b_ps[5]], writes=[b_rs])
                    k.op("dve", lambda h=h, tt=tt: nc.vector.tensor_tensor(
                        out=ox[:, h, tt * 512:(tt + 1) * 512], in0=bank(4), in1=rs, op=ALU.mult),
                        reads=[b_ps[4], b_rs], writes=[b_ox])
            resid_proj(es, ox, b_ox, 4, w_xo, lambda oc: X1_d[oc * 128:(oc + 1) * 128, :], b_X1, X2_d, b_X2)
            k.barrier()

    def phase_F():
        with ExitStack() as es:
            h2 = A(es, "h2", [128, 32, 1024], BF16)
            b_h2 = Buf("h2")
            with ExitStack() as es2:
                res = dict(xc=mkrot(es2, "xcf", 3, [128, 1024], F32), sq=mkrot(es2, "sqf", 2, [128, 1024], F32),
                           rstd=(A(es2, "rstdf", [128, 1024], F32), Buf("rstdf")))
                rmsnorm_T(X2_d, 1024, 3, res, (6, 7), dst3=h2, dst_buf=b_h2, src_buf=b_X2)
                k.barrier()
            wr = mkrot(es, "wup", 3, [128, 32, 256], BF16)
            rl = mkrot(es, "rl", 2, [128, 1024], F32)
            ao = mkrot(es, "ao", 3, [128, 1024], BF16)
            prs = Rot([0, 2, 4])
            for t in range(DFF // 256):
                w3, bw = wr.next()
                k.dma("pool", w3, wview(w_up, 0, D, t * 256, 256), writes=[bw])
                for cch in range(2):
                    fc = t * 2 + cch
                    pb = prs.next()
                    proj(w3, bw, 32, cch * 128, lambda kc, o, n: h2[:, kc, o:o + n], [b_h2], NT_OWN, (pb, pb + 1))
                    r_, br = rl.next()
                    k.op("act", lambda r_=r_, pb=pb: nc.scalar.activation(out=r_, in_=pair(pb), func=AF.Relu),
                         reads=[b_ps[pb], b_ps[pb + 1]], writes=[br])
                    a_, ba = ao.next()
                    k.op("dve", lambda r_=r_, a_=a_: nc.vector.tensor_tensor(out=a_, in0=r_, in1=r_, op=ALU.mult),
                         reads=[br], writes=[ba])
                    k.dma("sp", A_d[fc * 128:(fc + 1) * 128, :], a_, reads=[ba], writes=[b_A])
            k.barrier()
        with ExitStack() as es:
            ah = A(es, "ah", [128, 64, 1024], BF16)
            b_ah = Buf("ah")
            wr = mkrot(es, "wdn", 2, [128, 64, 128], BF16)
            xr = mkrot(es, "xrf", 2, [128, 1024], F32)
            pr = mkrot(es, "prf", 2, [128, 1024], F32)
            prs = Rot([0, 2, 4, 6])
            for kh in range(2):
                for c0 in range(0, 64, 8):
                    r0 = kh * 8192 + c0 * 128
                    k.dma("sp", ah[:, c0:c0 + 8, :], A_d[r0:r0 + 1024, :].rearrange("(c p) n -> p c n", p=128),
                          reads=[b_A], writes=[b_ah])
                for oc in range(32):
                    w3, bw = wr.next()
                    k.dma("pool", w3, wview(w_dn, kh * 8192, 8192, oc * 128, 128), writes=[bw])
                    pb = prs.next()
                    proj(w3, bw, 64, 0, lambda kc, o, n: ah[:, kc, o:o + n], [b_ah], NT_OWN, (pb, pb + 1))
                    p_, bp = pr.next()
                    if kh == 0:
                        copy_evac(alt(), p_, pair(pb), [b_ps[pb], b_ps[pb + 1]], [bp])
                        k.dma("sp", P_d[oc * 128:(oc + 1) * 128, :], p_, reads=[bp], writes=[b_P])
                    else:
                        x_, bx = xr.next()
                        k.dma("sp", p_, P_d[oc * 128:(oc + 1) * 128, :], reads=[b_P], writes=[bp])
                        k.dma("sp", x_, X2_d[oc * 128:(oc + 1) * 128, :], reads=[b_X2], writes=[bx])
                        k.op("dve", lambda p_=p_, pb=pb: nc.vector.tensor_tensor(out=p_, in0=pair(pb), in1=p_, op=ALU.add),
                             reads=[b_ps[pb], b_ps[pb + 1], bp], writes=[bp])
                        k.op("dve", lambda p_=p_, x_=x_: nc.vector.tensor_tensor(out=x_, in0=p_, in1=x_, op=ALU.add),
                             reads=[bp, bx], writes=[bx])
                        k.dma("sp", xout[oc * 128:(oc + 1) * 128, :], x_, reads=[bx], writes=[b_xout])
            k.barrier()

    def phase_G():
        with ExitStack() as es:
            res = dict(xc=mkrot(es, "xcg", 3, [128, 1024], F32), sq=mkrot(es, "sqg", 2, [128, 1024], F32),
                       rstd=(A(es, "rstdg", [128, 1024], F32), Buf("rstdg")), of=mkrot(es, "ofg", 3, [128, 1024], F32))
            rmsnorm_T(xout, 1024, 4, res, (6, 7), src_buf=b_xout, out_d=yn, out_buf=b_yn)
            k.barrier()

    phases = dict(K=phase_K, P=phase_P, A=phase_A, C=phase_C, D=phase_D, E=phase_E, F=phase_F, G=phase_G)
    for ph in PHASES:
        phases[ph]()
    k.barrier()
    return nc, k


_PROG = {}


def _get_prog(phases="KPACDEFG", debug_out=()):
    key = (phases, tuple(debug_out))
    if key not in _PROG:
        _PROG[key] = build_program(phases, debug_out)
    return _PROG[key]


def _t5_bucket_np(dist):
    n = np.maximum(dist, 0)
    nf = np.maximum(n, 1).astype(np.float32)
    large = 16 + (np.log(nf / np.float32(16)) / np.float32(math.log(128 / 16)) * np.float32(16)).astype(np.int32)
    large = np.minimum(large, 31)
    return np.where(n < 16, n, large)


def _own_idx(c):
    return np.concatenate([np.arange((8 * j + c) * 128, (8 * j + c + 1) * 128) for j in range(8)])


def _consts(rel_bias):
    rel_bias = np.asarray(rel_bias, np.float32)
    s = np.arange(128)[:, None]
    t = np.arange(128)[None, :]
    biasn = np.zeros((128, 2, 16, 128), np.float32)
    for off in range(2):
        bk = _t5_bucket_np(t - s + 128 * off)
        biasn[:, off, :, :] = rel_bias[bk].transpose(0, 2, 1)
    bfar = np.ascontiguousarray(np.broadcast_to(rel_bias[31][None, :], (128, 16))).astype(np.float32)
    identrep = np.ascontiguousarray(np.tile(np.eye(128, dtype=np.float32), (1, 4)))
    tt = np.arange(128)[:, None]
    ss = np.arange(128)[None, :]
    cmask = np.where(ss > tt, np.float32(BIG), np.float32(0)).astype(np.float32)
    return dict(biasn=biasn.reshape(128, -1), bfar=bfar, identrep=identrep, cmask=cmask)


def _layer_in_maps(l, xfull, inp, consts, names):
    xT = np.ascontiguousarray(xfull.T)
    g = np.stack([inp["norm_mix"][l], inp["norm_xattn"][l], inp["norm_mem"][l], inp["norm_mlp"][l], inp["norm_final"]])
    gains = np.ascontiguousarray(g.reshape(5, 32, 128).transpose(2, 0, 1).reshape(128, 160)).astype(np.float32)
    convw = np.ascontiguousarray(inp["conv_w"][l].reshape(3, 16, 128).transpose(2, 1, 0).reshape(128, 48)).astype(np.float32)
    shared = dict(
        w_in=inp["w_in"][l], convw=convw, w_ca=inp["w_conv_out"][l], w_ba=inp["w_attn_out"][l],
        w_mo=inp["w_mix_out"][l], w_xq=inp["w_xq"][l], w_xkv=inp["w_xkv"][l], w_xo=inp["w_xo"][l],
        w_up=inp["w_up"][l], w_dn=inp["w_down"][l], gains=gains,
        memT=np.ascontiguousarray(inp["mem"][0].T), **consts)
    maps = []
    for c in range(NCORE):
        m = {}
        if "xo" in names:
            xo = np.zeros((D, NH), np.float32)
            xo[:, :NOWN] = xT[:, _own_idx(c)]
            for j in range(8):
                t0 = (8 * j + c) * 128
                if t0 >= 2:
                    xo[:, NOWN + 2 * j:NOWN + 2 * j + 2] = xT[:, t0 - 2:t0]
            m["xo"] = xo
        if "xall" in names:
            xa = np.zeros((D, S), np.float32)
            sh = (7 - c) * 128
            xa[:, sh:] = xT[:, :S - sh]
            m["xall"] = xa
        if "padmask" in names:
            pm = np.zeros((128, 1024), np.float32)
            pm[:, :(7 - c) * 128] = BIG
            m["padmask"] = pm
        for n in names:
            if n not in m:
                m[n] = np.ascontiguousarray(np.asarray(shared[n], np.float32))
        maps.append(m)
    return maps


def kernel(**inputs):
    inp = {k_: np.asarray(v) for k_, v in inputs.items()}
    nc, kb = _get_prog()
    names = kb.in_names
    consts = _consts(inp["rel_bias"])
    x = np.asarray(inp["x"][0], np.float32)
    out = None
    for l in range(2):
        maps = _layer_in_maps(l, x, inp, consts, names)
        res = run_bass_kernel_spmd(nc, maps, core_ids=list(range(NCORE)))
        del maps
        key = "xout" if l == 0 else "yn"
        nxt = np.empty((S, D), np.float32)
        for c in range(NCORE):
            nxt[_own_idx(c)] = res.results[c][key].T
        x = nxt
    return x[None].astype(np.float32)
```

```python
import math
import numpy as np
import concourse.bass as bass
import concourse.mybir as mybir
from concourse.bass_utils import run_bass_kernel_spmd

F32 = mybir.dt.float32
BF16 = mybir.dt.bfloat16
AF = mybir.ActivationFunctionType
ALU = mybir.AluOpType
AX = mybir.AxisListType

D = 4096
S = 8192
NCORE = 8
NOWN = 1024
NH = 1040
CONV = 2048
IN_DIM = 19552
OFF_U, OFF_CB, OFF_CC, OFF_Q, OFF_K, OFF_V = 0, 2048, 4096, 6144, 8192, 8704
OFF_IQ, OFF_IK, OFF_IW, OFF_GA, OFF_GB = 9216, 11264, 11328, 11360, 15456
DFF = 16384
EPS = 1e-6
TOPK = 256
NEG = -30000.0
BIG = -1.0e30
ATT_SCALE = 128 ** -0.5
IW_SCALE = (32 ** -0.5) * (64 ** -0.5)
NBIS = 16


class Buf:
    __slots__ = ("name", "w", "r")

    def __init__(self, name):
        self.name = name
        self.w = None
        self.r = {}


class KB:
    COMPUTE = ("pe", "act", "dve", "pool")
    NPOOL = 12
    LIMIT = 30000

    def __init__(self, nc):
        self.nc = nc
        self.eng = {"pe": nc.tensor, "act": nc.scalar, "dve": nc.vector, "pool": nc.gpsimd, "sp": nc.sync}
        self.nsem = 0
        self.sem = {e: self._newsem(e) for e in self.COMPUTE}
        self.cnt = {e: 0 for e in self.COMPUTE}
        self.waited = {e: {} for e in self.eng}
        self.dsem = {q: [self._newsem("d" + q) for _ in range(self.NPOOL)] for q in ("sp", "pool")}
        self.dcnt = {q: [0] * self.NPOOL for q in ("sp", "pool")}
        self.drr = {q: 0 for q in ("sp", "pool")}
        self.nins = 0

    def _newsem(self, tag):
        self.nsem += 1
        return self.nc.alloc_semaphore(f"sm_{tag}_{self.nsem}")

    def _wait(self, e, tick):
        te, sem, val = tick
        key = id(sem)
        if self.waited[e].get(key, 0) >= val:
            return
        self.eng[e].wait_ge(sem, val)
        self.waited[e][key] = val

    def _deps(self, e, reads, writes):
        for b in reads:
            if b.w is not None:
                t = b.w
                if not (t[0] == e and e == "pe"):
                    self._wait(e, t)
        for b in writes:
            if b.w is not None:
                t = b.w
                if not (t[0] == e and e == "pe"):
                    self._wait(e, t)
            for t in b.r.values():
                if t[0] == e:
                    continue
                self._wait(e, t)

    def _mark(self, tick, reads, writes):
        for b in writes:
            b.w = tick
            b.r = {}
        for b in reads:
            b.r[(tick[0], id(tick[1]))] = tick

    def op(self, e, fn, reads=(), writes=(), inc=True):
        self._deps(e, reads, writes)
        ins = fn()
        self.nins += 1
        if inc:
            if self.cnt[e] >= self.LIMIT:
                self.sem[e] = self._newsem(e)
                self.cnt[e] = 0
            self.cnt[e] += 1
            ins.then_inc(self.sem[e], 1)
            tick = (e, self.sem[e], self.cnt[e])
        else:
            tick = (e, self.sem[e], self.cnt[e] + 1)
        self._mark(tick, reads, writes)
        return tick

    def dma(self, q, out, in_, reads=(), writes=()):
        if isinstance(in_, LazyIn):
            in_ = in_.ap()
        self._deps(q, reads, writes)
        i = self.drr[q] % self.NPOOL
        self.drr[q] += 1
        sem = self.dsem[q][i]
        if self.dcnt[q][i] > 0:
            self._wait(q, ("dma", sem, 16 * self.dcnt[q][i]))
        self.eng[q].dma_start(out=out, in_=in_).then_inc(sem, 16)
        self.nins += 1
        self.dcnt[q][i] += 1
        tick = ("dma", sem, 16 * self.dcnt[q][i])
        self._mark(tick, reads, writes)
        return tick

    def all_ticks(self):
        ts = []
        for e in self.COMPUTE:
            if self.cnt[e] > 0:
                ts.append((e, self.sem[e], self.cnt[e]))
        for q in self.dsem:
            for i, sem in enumerate(self.dsem[q]):
                if self.dcnt[q][i] > 0:
                    ts.append(("dma", sem, 16 * self.dcnt[q][i]))
        return ts

    def barrier(self, engines=None):
        ts = self.all_ticks()
        for e in (engines or self.eng):
            for t in ts:
                self._wait(e, t)


class LazyIn:
    def __init__(self, nc, name, shape, reg):
        self.nc, self.name, self.shape, self.reg, self._ap = nc, name, shape, reg, None

    def ap(self):
        if self._ap is None:
            self._ap = self.nc.dram_tensor(self.name, list(self.shape), F32, kind="ExternalInput").ap()
            self.reg.append(self.name)
        return self._ap

    def __getitem__(self, idx):
        return self.ap()[idx]

    def rearrange(self, *a, **kw):
        return self.ap().rearrange(*a, **kw)


class Rot:
    def __init__(self, items):
        self.items = items
        self.i = 0

    def next(self):
        it = self.items[self.i % len(self.items)]
        self.i += 1
        return it


def build_program(PHASES="KPACDEFG", DEBUG_OUT=()):
    nc = bass.Bass("TRN2", target_bir_lowering=False)
    k = KB(nc)

    in_names = []
    k.in_names = in_names

    def din(name, shape, dt=F32):
        return LazyIn(nc, name, shape, in_names)

    def dscr(name, shape, dt):
        if name in DEBUG_OUT:
            return nc.dram_tensor(name, list(shape), dt, kind="ExternalOutput").ap()
        return nc.dram_tensor(name, list(shape), dt).ap()

    xo = din("xo", [D, NH])
    xall = din("xall", [D, S])
    padmask_d = din("padmask", [128, 1024])
    w_in = din("w_in", [D, IN_DIM])
    convw_d = din("convw", [128, 16 * 3])
    w_ca = din("w_ca", [CONV, D])
    w_ba = din("w_ba", [2048, D])
    w_mo = din("w_mo", [D, D])
    w_xq = din("w_xq", [D, 512])
    w_xkv = din("w_xkv", [D, 1024])
    w_xo = din("w_xo", [512, D])
    w_up = din("w_up", [D, DFF])
    w_dn = din("w_dn", [DFF, D])
    gains_d = din("gains", [128, 5 * 32])
    memT = din("memT", [D, 256])
    biasn_d = din("biasn", [128, 2 * 16 * 128])
    bfar_d = din("bfar", [128, 16])
    identrep_d = din("identrep", [128, 512])
    cmask_d = din("cmask", [128, 128])
    xout = nc.dram_tensor("xout", [D, NOWN], F32, kind="ExternalOutput").ap()
    yn = nc.dram_tensor("yn", [D, NOWN], F32, kind="ExternalOutput").ap()

    KT_d = dscr("KT_d", [4, 128, S], BF16)
    V_d = dscr("V_d", [S, 512], BF16)
    IK_d = dscr("IK_d", [64, S], BF16)
    QT_d = dscr("QT_d", [2048, NOWN], BF16)
    IQT_d = dscr("IQT_d", [2048, NOWN], BF16)
    YAIN_d = dscr("YAIN_d", [2048, NOWN], BF16)
    OT_d = dscr("OT_d", [2048, NOWN], BF16)
    SGA_d = dscr("SGA_d", [D, NOWN], F32)
    SGB_d = dscr("SGB_d", [D, NOWN], F32)
    M_d = dscr("M_d", [D, NOWN], BF16)
    X1_d = dscr("X1_d", [D, NOWN], F32)
    X2_d = dscr("X2_d", [D, NOWN], F32)
    A_d = dscr("A_d", [DFF, NOWN], BF16)
    P_d = dscr("P_d", [D, NOWN], F32)
    b_KT, b_V, b_IK = Buf("KT_d"), Buf("V_d"), Buf("IK_d")
    b_QT, b_IQT, b_YAIN, b_OT = Buf("QT_d"), Buf("IQT_d"), Buf("YAIN_d"), Buf("OT_d")
    b_SGA, b_SGB, b_M, b_X1, b_X2, b_A, b_P = (Buf(n) for n in ("SGA", "SGB", "M", "X1", "X2", "A", "P"))
    b_xout = Buf("xout")
    b_yn = Buf("yn")

    psum = nc.alloc_psum_tensor("psum", [128, 8 * 512], F32).ap()

    def bank(i, n=512):
        return psum[:, i * 512:i * 512 + n]

    b_ps = [Buf(f"ps{i}") for i in range(8)]

    def sb(name, shape, dt):
        return nc.alloc_sbuf_tensor("sb_" + name, list(shape), dt).ap()

    gains = sb("gains", [128, 5 * 32], F32)
    convw = sb("convw", [128, 48], F32)
    ones_f = sb("ones_f", [128, 128], F32)
    ones_b = sb("ones_b", [128, 128], BF16)
    iw_sb = sb("iw_sb", [128, 8 * 32], F32)
    b_const = Buf("const")
    b_iw = Buf("iw")
    k.dma("sp", gains, gains_d, writes=[b_const])
    k.dma("sp", convw, convw_d, writes=[b_const])
    k.op("dve", lambda: nc.vector.memset(ones_f, 1.0), writes=[b_const])
    k.op("dve", lambda: nc.vector.memset(ones_b, 1.0), writes=[b_const])

    from contextlib import ExitStack

    uid = [0]

    def A(es, name, shape, dt):
        uid[0] += 1
        t = es.enter_context(nc.sbuf_tensor(f"t{uid[0]}_" + name, list(shape), dt))
        return t.ap() if hasattr(t, "ap") else t[:]

    def mkrot(es, name, n, shape, dt):
        return Rot([(A(es, f"{name}{i}", shape, dt), Buf(f"{name}{i}")) for i in range(n)])

    flip = [0]

    def alt():
        flip[0] ^= 1
        return "act" if flip[0] else "dve"

    def copy_evac(e, out, in_, reads, writes):
        if e == "act":
            return k.op("act", lambda: nc.scalar.copy(out=out, in_=in_), reads=reads, writes=writes)
        return k.op("dve", lambda: nc.vector.tensor_copy(out=out, in_=in_), reads=reads, writes=writes)

    def rmsnorm_T(src, ncols, gidx, res, ps_banks, dst3=None, dst_buf=None, src_buf=None, out_d=None, out_buf=None):
        ntiles = [(o, min(512, ncols - o)) for o in range(0, ncols, 512)]
        rd = [src_buf] if src_buf is not None else []
        rstd, b_rstd = res["rstd"]
        for c in range(32):
            xc, bx = res["xc"].next()
            k.dma("sp", xc[:, :ncols], src[c * 128:(c + 1) * 128, :], reads=rd, writes=[bx])
            sq, bq = res["sq"].next()
            k.op("act", lambda xc=xc, sq=sq: nc.scalar.activation(out=sq[:, :ncols], in_=xc[:, :ncols], func=AF.Square),
                 reads=[bx], writes=[bq])
            for ti, (o, n) in enumerate(ntiles):
                pb = ps_banks[ti]
                k.op("pe", lambda pb=pb, sq=sq, o=o, n=n, c=c: nc.tensor.matmul(
                    bank(pb, n), ones_f, sq[:, o:o + n], start=(c == 0), stop=(c == 31)),
                    reads=[bq, b_const], writes=[b_ps[pb]], inc=(c == 31 or ti == len(ntiles) - 1))
        for ti, (o, n) in enumerate(ntiles):
            pb = ps_banks[ti]
            k.op("dve", lambda pb=pb, o=o, n=n: nc.vector.tensor_scalar(
                out=rstd[:, o:o + n], in0=bank(pb, n), scalar1=1.0 / D, scalar2=EPS, op0=ALU.mult, op1=ALU.add),
                reads=[b_ps[pb]], writes=[b_rstd])
        k.op("act", lambda: nc.scalar.activation(out=rstd[:, :ncols], in_=rstd[:, :ncols], func=AF.Sqrt),
             reads=[b_rstd], writes=[b_rstd])
        k.op("dve", lambda: nc.vector.reciprocal(out=rstd[:, :ncols], in_=rstd[:, :ncols]),
             reads=[b_rstd], writes=[b_rstd])
        for c in range(32):
            xc, bx = res["xc"].next()
            k.dma("sp", xc[:, :ncols], src[c * 128:(c + 1) * 128, :], reads=rd, writes=[bx])
            gs = gains[:, gidx * 32 + c:gidx * 32 + c + 1]
            if out_d is None:
                k.op("dve", lambda xc=xc, c=c, gs=gs: nc.vector.scalar_tensor_tensor(
                    out=dst3[:, c, :ncols], in0=xc[:, :ncols], scalar=gs,
                    in1=rstd[:, :ncols], op0=ALU.mult, op1=ALU.mult),
                    reads=[bx, b_rstd, b_const], writes=[dst_buf])
            else:
                of, bo = res["of"].next()
                k.op("dve", lambda xc=xc, of=of, gs=gs: nc.vector.scalar_tensor_tensor(
                    out=of[:, :ncols], in0=xc[:, :ncols], scalar=gs,
                    in1=rstd[:, :ncols], op0=ALU.mult, op1=ALU.mult),
                    reads=[bx, b_rstd, b_const], writes=[bo])
                k.dma("sp", out_d[c * 128:(c + 1) * 128, :], of[:, :ncols], reads=[bo], writes=[out_buf])

    def wview(wd, r0, nrows, c0, ncols):
        return wd[r0:r0 + nrows, c0:c0 + ncols].rearrange("(c p) n -> p c n", p=128)

    def proj(w3, bw, nk, col0, rhs_fn, rhs_bufs, ntiles, pbanks, m=128):
        for kc in range(nk):
            for ti, (o, n) in enumerate(ntiles):
                pb = pbanks[ti]
                k.op("pe", lambda pb=pb, kc=kc, o=o, n=n: nc.tensor.matmul(
                    psum[0:m, pb * 512:pb * 512 + n], w3[:, kc, col0:col0 + m], rhs_fn(kc, o, n),
                    start=(kc == 0), stop=(kc == nk - 1)),
                    reads=[bw] + rhs_bufs, writes=[b_ps[pb]], inc=(kc == nk - 1))

    NT_OWN = [(0, 512), (512, 512)]
    NT_HALO = [(0, 512), (512, 512), (1024, 16)]

    def pair(pb):
        return psum[:, pb * 512:pb * 512 + 1024]

    def phase_K():
        with ExitStack() as es:
            wk = A(es, "wk", [128, 32, 512], BF16)
            wv = A(es, "wv", [128, 32, 512], BF16)
            wik = A(es, "wik", [128, 32, 64], BF16)
            bwk, bwv, bwik = Buf("wk"), Buf("wv"), Buf("wik")
            for c0 in (0, 256):
                k.dma("pool", wk[:, :, c0:c0 + 256], wview(w_in, 0, D, OFF_K + c0, 256), writes=[bwk])
            for c0 in (0, 256):
                k.dma("pool", wv[:, :, c0:c0 + 256], wview(w_in, 0, D, OFF_V + c0, 256), writes=[bwv])
            k.dma("pool", wik, wview(w_in, 0, D, OFF_IK, 64), writes=[bwik])
            hts = mkrot(es, "htk", 2, [128, 32, 512], BF16)
            res = dict(xc=mkrot(es, "xck", 3, [128, 512], F32), sq=mkrot(es, "sqk", 2, [128, 512], F32),
                       rstd=(A(es, "rstdk", [128, 512], F32), Buf("rstdk")))
            kto = mkrot(es, "kto", 2, [128, 4, 512], BF16)
            iko = mkrot(es, "iko", 2, [64, 512], BF16)
            vo = mkrot(es, "vo", 2, [128, 4, 512], BF16)
            KT_v = KT_d.rearrange("g p n -> p g n")
            V_v = V_d.rearrange("(b p) n -> p b n", p=128)
            prot = Rot([0, 1, 2, 3, 4, 5])
            for tt in range(S // 512):
                h3, bh = hts.next()
                rmsnorm_T(xall[:, tt * 512:(tt + 1) * 512], 512, 0, res, (7,), dst3=h3, dst_buf=bh)
                kt, bkt = kto.next()
                for g in range(4):
                    pb = prot.next()
                    proj(wk, bwk, 32, g * 128, lambda kc, o, n, h3=h3: h3[:, kc, o:o + n], [bh], [(0, 512)], (pb,))
                    copy_evac(alt(), kt[:, g, :], bank(pb), [b_ps[pb]], [bkt])
                k.dma("sp", KT_v[:, :, tt * 512:(tt + 1) * 512], kt, reads=[bkt], writes=[b_KT])
                ikt, bikt = iko.next()
                pb = prot.next()
                proj(wik, bwik, 32, 0, lambda kc, o, n, h3=h3: h3[:, kc, o:o + n], [bh], [(0, 512)], (pb,), m=64)
                copy_evac(alt(), ikt, psum[0:64, pb * 512:(pb + 1) * 512], [b_ps[pb]], [bikt])
                k.dma("sp", IK_d[:, tt * 512:(tt + 1) * 512], ikt, reads=[bikt], writes=[b_IK])
                vt, bvt = vo.next()
                for s4 in range(4):
                    pb = prot.next()
                    for kc in range(32):
                        k.op("pe", lambda pb=pb, kc=kc, s4=s4, h3=h3: nc.tensor.matmul(
                            bank(pb), h3[:, kc, s4 * 128:(s4 + 1) * 128], wv[:, kc, :], start=(kc == 0), stop=(kc == 31)),
                            reads=[bwv, bh], writes=[b_ps[pb]], inc=(kc == 31))
                    copy_evac(alt(), vt[:, s4, :], bank(pb), [b_ps[pb]], [bvt])
                k.dma("sp", V_v[:, tt * 4:(tt + 1) * 4, :], vt, reads=[bvt], writes=[b_V])
            k.barrier()

    def phase_P():
        with ExitStack() as es:
            h3 = A(es, "hTo", [128, 32, NH], BF16)
            bh = Buf("hTo")
            with ExitStack() as es2:
                res = dict(xc=mkrot(es2, "xcp", 3, [128, NH], F32), sq=mkrot(es2, "sqp", 2, [128, NH], F32),
                           rstd=(A(es2, "rstdp", [128, NH], F32), Buf("rstdp")))
                rmsnorm_T(xo, NH, 0, res, (5, 6, 7), dst3=h3, dst_buf=bh)
                k.barrier()
            wrot = mkrot(es, "wp", 2, [128, 32, 384], BF16)
            rhs_h = lambda kc, o, n: h3[:, kc, o:o + n]
            uf = A(es, "uf", [128, NH], F32)
            cu = A(es, "cu", [128, 8, 130], F32)
            cv = A(es, "cv", [128, 8, 128], F32)
            b_uf, b_cu, b_cv = Buf("uf"), Buf("cu"), Buf("cv")
            yo = mkrot(es, "yo", 2, [128, 1024], BF16)
            for i in range(16):
                w3, bw = wrot.next()
                for si, off in enumerate((OFF_U, OFF_CB, OFF_CC)):
                    k.dma("pool", w3[:, :, si * 128:(si + 1) * 128], wview(w_in, 0, D, off + i * 128, 128), writes=[bw])
                proj(w3, bw, 32, 0, rhs_h, [bh], NT_HALO, (0, 1, 4))
                k.op("act", lambda: nc.scalar.copy(out=uf[:, 0:1024], in_=pair(0)), reads=[b_ps[0], b_ps[1]], writes=[b_uf])
                k.op("act", lambda: nc.scalar.copy(out=uf[:, 1024:1040], in_=bank(4, 16)), reads=[b_ps[4]], writes=[b_uf])
                proj(w3, bw, 32, 256, rhs_h, [bh], NT_HALO, (2, 3, 5))
                k.op("dve", lambda: nc.vector.tensor_tensor(
                    out=cu[:, :, 2:130], in0=pair(2).rearrange("p (j t) -> p j t", j=8),
                    in1=uf[:, 0:1024].rearrange("p (j t) -> p j t", j=8), op=ALU.mult),
                    reads=[b_ps[2], b_ps[3], b_uf], writes=[b_cu])
                k.op("dve", lambda: nc.vector.tensor_tensor(
                    out=cu[:, :, 0:2], in0=bank(5, 16).rearrange("p (j t) -> p j t", j=8),
                    in1=uf[:, 1024:1040].rearrange("p (j t) -> p j t", j=8), op=ALU.mult),
                    reads=[b_ps[5], b_uf], writes=[b_cu])
                k.op("dve", lambda i=i: nc.vector.tensor_scalar(
                    out=cv, in0=cu[:, :, 2:130], scalar1=convw[:, i * 3 + 2:i * 3 + 3], scalar2=None, op0=ALU.mult),
                    reads=[b_cu, b_const], writes=[b_cv])
                k.op("dve", lambda i=i: nc.vector.scalar_tensor_tensor(
                    out=cv, in0=cu[:, :, 1:129], scalar=convw[:, i * 3 + 1:i * 3 + 2], in1=cv, op0=ALU.mult, op1=ALU.add),
                    reads=[b_cu, b_cv, b_const], writes=[b_cv])
                k.op("dve", lambda i=i: nc.vector.scalar_tensor_tensor(
                    out=cv, in0=cu[:, :, 0:128], scalar=convw[:, i * 3:i * 3 + 1], in1=cv, op0=ALU.mult, op1=ALU.add),
                    reads=[b_cu, b_cv, b_const], writes=[b_cv])
                proj(w3, bw, 32, 128, rhs_h, [bh], NT_OWN, (6, 7))
                y, by = yo.next()
                k.op("dve", lambda y=y: nc.vector.tensor_tensor(
                    out=y, in0=pair(6), in1=cv.rearrange("p j t -> p (j t)"), op=ALU.mult),
                    reads=[b_ps[6], b_ps[7], b_cv], writes=[by])
                k.dma("sp", YAIN_d[i * 128:(i + 1) * 128, :], y, reads=[by], writes=[b_YAIN])
            ob = mkrot(es, "obp", 3, [128, 1024], BF16)
            prs = Rot([0, 2, 4, 6])
            for off, dst, bdst in ((OFF_Q, QT_d, b_QT), (OFF_IQ, IQT_d, b_IQT)):
                for t in range(8):
                    w3, bw = wrot.next()
                    k.dma("pool", w3[:, :, 0:256], wview(w_in, 0, D, off + t * 256, 256), writes=[bw])
                    for cch in range(2):
                        pb = prs.next()
                        proj(w3, bw, 32, cch * 128, rhs_h, [bh], NT_OWN, (pb, pb + 1))
                        o_, bo = ob.next()
                        copy_evac(alt(), o_, pair(pb), [b_ps[pb], b_ps[pb + 1]], [bo])
                        r0 = (t * 2 + cch) * 128
                        k.dma("sp", dst[r0:r0 + 128, :], o_, reads=[bo], writes=[bdst])
            w3, bw = wrot.next()
            k.dma("pool", w3[:, :, 0:32], wview(w_in, 0, D, OFF_IW, 32), writes=[bw])
            pb = prs.next()
            for j in range(8):
                for kc in range(32):
                    k.op("pe", lambda j=j, kc=kc, pb=pb, w3=w3: nc.tensor.matmul(
                        psum[:, pb * 512 + j * 32:pb * 512 + (j + 1) * 32], h3[:, kc, j * 128:(j + 1) * 128],
                        w3[:, kc, 0:32], start=(kc == 0), stop=(kc == 31)),
                        reads=[bw, bh], writes=[b_ps[pb]], inc=(kc == 31))
            k.op("dve", lambda pb=pb: nc.vector.tensor_scalar(
                out=iw_sb, in0=bank(pb, 256), scalar1=IW_SCALE, scalar2=None, op0=ALU.mult),
                reads=[b_ps[pb]], writes=[b_iw])
            og = mkrot(es, "ogp", 3, [128, 1024], F32)
            for off, dst, bdst in ((OFF_GA, SGA_d, b_SGA), (OFF_GB, SGB_d, b_SGB)):
                for t in range(16):
                    w3, bw = wrot.next()
                    k.dma("pool", w3[:, :, 0:256], wview(w_in, 0, D, off + t * 256, 256), writes=[bw])
                    for cch in range(2):
                        pb = prs.next()
                        proj(w3, bw, 32, cch * 128, rhs_h, [bh], NT_OWN, (pb, pb + 1))
                        o_, bo = og.next()
                        k.op("act", lambda o_=o_, pb=pb: nc.scalar.activation(out=o_, in_=pair(pb), func=AF.Sigmoid),
                             reads=[b_ps[pb], b_ps[pb + 1]], writes=[bo])
                        r0 = (t * 2 + cch) * 128
                        k.dma("sp", dst[r0:r0 + 128, :], o_, reads=[bo], writes=[bdst])
            k.barrier()

    def phase_A():
        with ExitStack() as es:
            identrep = A(es, "identrep", [128, 512], BF16)
            cmask = A(es, "cmask", [128, 128], F32)
            padmask = A(es, "padmask", [128, 1024], F32)
            biasn = A(es, "biasn", [128, 2 * 16 * 128], F32)
            bfar = A(es, "bfar", [128, 16], F32)
            b_ac = Buf("attnconst")
            k.dma("sp", cmask, cmask_d, writes=[b_ac])
            k.dma("sp", padmask, padmask_d, writes=[b_ac])
            k.dma("sp", biasn, biasn_d, writes=[b_ac])
            k.dma("sp", bfar, bfar_d, writes=[b_ac])
            k.dma("pool", identrep, identrep_d, writes=[b_ac])
            for off in range(2):
                for h in range(16):
                    sl = biasn[:, (off * 16 + h) * 128:(off * 16 + h + 1) * 128]
                    k.op("dve", lambda sl=sl, h=h: nc.vector.tensor_scalar(
                        out=sl, in0=sl, scalar1=bfar[:, h:h + 1], scalar2=None, op0=ALU.subtract),
                        reads=[b_ac], writes=[b_ac])
            sc = A(es, "sc", [128, S], F32)
            b_sc = [Buf(f"sc{i}") for i in range(16)]
            negm = A(es, "negm", [128, S], BF16)
            b_negm = Buf("negm")
            qTs = mkrot(es, "qT", 2, [128, 2048], BF16)
            iqTs = mkrot(es, "iqT", 2, [128, 16, 128], BF16)
            ik2 = mkrot(es, "ik2", 3, [128, 512], BF16)
            rr = mkrot(es, "rr", 3, [128, 512], BF16)
            sm = A(es, "sm", [128, 16], F32)
            b_sm = Buf("sm")
            mx, mn, lo, hi, mid, cnt, ge, d1, half = (sm[:, i:i + 1] for i in range(9))
            k.op("dve", lambda: nc.vector.memset(half, 0.5), writes=[b_sm])
            ktc = mkrot(es, "ktc", 2, [128, 1024], BF16)
            vc = mkrot(es, "vc", 2, [128, 8, 128], BF16)
            pp = mkrot(es, "pp", 3, [128, 512], BF16)
            tmpf = mkrot(es, "tmpf", 2, [128, 512], F32)
            rs = A(es, "rs", [128, 512], F32)
            b_rs = Buf("rs")
            oo = mkrot(es, "oo", 2, [128, 512], BF16)
            QT_v = QT_d.rearrange("(h p) n -> p h n", p=128)
            IQT_v = IQT_d.rearrange("(h p) n -> p h n", p=128)
            OT_v = OT_d.rearrange("(h p) n -> p h n", p=128)
            V_v = V_d.rearrange("(b p) n -> p b n", p=128)
            irot = Rot([0, 1, 2, 3])
            lrot = Rot([6, 7])
            for j in range(8):
                L = 1024 * (j + 1)
                nkb = 8 * (j + 1)
                qT, bq = qTs.next()
                iqT, biq = iqTs.next()
                k.dma("sp", qT.rearrange("p (h t) -> p h t", h=16), QT_v[:, :, j * 128:(j + 1) * 128], reads=[b_QT], writes=[bq])
                k.dma("sp", iqT, IQT_v[:, :, j * 128:(j + 1) * 128], reads=[b_IQT], writes=[biq])
                for c5 in range(2 * (j + 1)):
                    ikt, bik = ik2.next()
                    k.dma("sp", ikt[0:64, :], IK_d[:, c5 * 512:(c5 + 1) * 512], reads=[b_IK], writes=[bik])
                    k.dma("sp", ikt[64:128, :], IK_d[:, c5 * 512:(c5 + 1) * 512], reads=[b_IK], writes=[bik])
                    scs = sc[:, c5 * 512:(c5 + 1) * 512]
                    for h in range(32):
                        ch, hf = h // 2, h % 2
                        pb = irot.next()
                        k.op("pe", lambda pb=pb, ch=ch, hf=hf, iqT=iqT, ikt=ikt: nc.tensor.matmul(
                            bank(pb), iqT[hf * 64:(hf + 1) * 64, ch, :], ikt[hf * 64:(hf + 1) * 64, :], start=True, stop=True),
                            reads=[biq, bik], writes=[b_ps[pb]])
                        r_, br = rr.next()
                        k.op("act", lambda pb=pb, r_=r_: nc.scalar.activation(out=r_, in_=bank(pb), func=AF.Relu),
                             reads=[b_ps[pb]], writes=[br])
                        iws = iw_sb[:, j * 32 + h:j * 32 + h + 1]
                        if h == 0:
                            k.op("dve", lambda r_=r_, iws=iws, scs=scs: nc.vector.tensor_scalar(
                                out=scs, in0=r_, scalar1=iws, scalar2=None, op0=ALU.mult),
                                reads=[br, b_iw], writes=[b_sc[c5]])
                        else:
                            k.op("dve", lambda r_=r_, iws=iws, scs=scs: nc.vector.scalar_tensor_tensor(
                                out=scs, in0=r_, scalar=iws, in1=scs, op0=ALU.mult, op1=ALU.add),
                                reads=[br, b_iw, b_sc[c5]], writes=[b_sc[c5]])
                scb = b_sc[:2 * (j + 1)]
                scL = sc[:, :L]
                k.op("dve", lambda scL=scL: nc.vector.tensor_reduce(out=mx, in_=scL, axis=AX.X, op=ALU.max), reads=scb, writes=[b_sm])
                k.op("dve", lambda scL=scL: nc.vector.tensor_reduce(out=mn, in_=scL, axis=AX.X, op=ALU.min), reads=scb, writes=[b_sm])
                k.op("dve", lambda: nc.vector.tensor_scalar(out=lo, in0=mn, scalar1=-1.0, scalar2=None, op0=ALU.add), reads=[b_sm], writes=[b_sm])
                k.op("dve", lambda: nc.vector.tensor_copy(out=hi, in_=mx), reads=[b_sm], writes=[b_sm])
                k.op("dve", lambda: nc.vector.tensor_tensor(out=sc[:, 0:1024], in0=sc[:, 0:1024], in1=padmask, op=ALU.add),
                     reads=[b_sc[0], b_sc[1], b_ac], writes=[b_sc[0], b_sc[1]])
                k.op("dve", lambda L=L: nc.vector.tensor_tensor(out=sc[:, L - 128:L], in0=sc[:, L - 128:L], in1=cmask, op=ALU.add),
                     reads=[b_sc[2 * j + 1], b_ac], writes=[b_sc[2 * j + 1]])
                for it in range(NBIS):
                    k.op("dve", lambda: nc.vector.scalar_tensor_tensor(out=mid, in0=lo, scalar=hi, in1=half, op0=ALU.add, op1=ALU.mult),
                         reads=[b_sm], writes=[b_sm])
                    k.op("dve", lambda scL=scL, L=L: nc.vector.tensor_scalar(
                        out=negm[:, :L], in0=scL, scalar1=mid, scalar2=0.0, op0=ALU.is_ge, op1=ALU.add, accum_out=cnt),
                        reads=scb + [b_sm], writes=[b_negm, b_sm])
                    k.op("dve", lambda: nc.vector.tensor_scalar(out=ge, in0=cnt, scalar1=TOPK - 0.5, scalar2=None, op0=ALU.is_ge),
                         reads=[b_sm], writes=[b_sm])
                    k.op("dve", lambda: nc.vector.tensor_tensor(out=d1, in0=mid, in1=lo, op=ALU.subtract), reads=[b_sm], writes=[b_sm])
                    k.op("dve", lambda: nc.vector.scalar_tensor_tensor(out=lo, in0=d1, scalar=ge, in1=lo, op0=ALU.mult, op1=ALU.add),
                         reads=[b_sm], writes=[b_sm])
                    k.op("dve", lambda: nc.vector.tensor_tensor(out=d1, in0=hi, in1=mid, op=ALU.subtract), reads=[b_sm], writes=[b_sm])
                    k.op("dve", lambda: nc.vector.scalar_tensor_tensor(out=hi, in0=d1, scalar=ge, in1=mid, op0=ALU.mult, op1=ALU.add),
                         reads=[b_sm], writes=[b_sm])
                k.op("dve", lambda scL=scL, L=L: nc.vector.tensor_scalar(
                    out=negm[:, :L], in0=scL, scalar1=lo, scalar2=NEG, op0=ALU.is_lt, op1=ALU.mult),
                    reads=scb + [b_sm], writes=[b_negm])
                for g in range(4):
                    qg = qT[:, g * 512:(g + 1) * 512]
                    pend = None
                    for chn in range(j + 1):
                        kt, bkt = ktc.next()
                        vt, bvt = vc.next()
                        k.dma("sp", kt, KT_d[g, :, chn * 1024:(chn + 1) * 1024], reads=[b_KT], writes=[bkt])
                        k.dma("sp", vt, V_v[:, chn * 8:(chn + 1) * 8, g * 128:(g + 1) * 128], reads=[b_V], writes=[bvt])
                        for b in range(8):
                            kb = chn * 8 + b
                            pl = lrot.next()
                            k.op("pe", lambda pl=pl, kt=kt, b=b, qg=qg: nc.tensor.matmul(
                                bank(pl), kt[:, b * 128:(b + 1) * 128], qg, start=True, stop=False),
                                reads=[bkt, bq], writes=[b_ps[pl]], inc=False)
                            k.op("pe", lambda pl=pl, kb=kb: nc.tensor.matmul(
                                bank(pl), negm[:, kb * 128:(kb + 1) * 128], identrep, start=False, stop=True),
                                reads=[b_negm, b_ac], writes=[b_ps[pl]])
                            p_, bp = pp.next()
                            if kb >= nkb - 2:
                                off = 0 if kb == nkb - 1 else 1
                                tf, btf = tmpf.next()
                                bsl = biasn[:, (off * 16 + 4 * g) * 128:(off * 16 + 4 * g + 4) * 128]
                                k.op("dve", lambda pl=pl, tf=tf, bsl=bsl: nc.vector.scalar_tensor_tensor(
                                    out=tf, in0=bank(pl), scalar=ATT_SCALE, in1=bsl, op0=ALU.mult, op1=ALU.add),
                                    reads=[b_ps[pl], b_ac], writes=[btf])
                                k.op("act", lambda tf=tf, p_=p_: nc.scalar.activation(out=p_, in_=tf, func=AF.Exp),
                                     reads=[btf], writes=[bp])
                            else:
                                k.op("act", lambda pl=pl, p_=p_: nc.scalar.activation(out=p_, in_=bank(pl), func=AF.Exp, scale=ATT_SCALE),
                                     reads=[b_ps[pl]], writes=[bp])
                            if pend is not None:
                                pend()
                            def pv(vt=vt, b=b, p_=p_, bp=bp, bvt=bvt, kb=kb):
                                k.op("pe", lambda: nc.tensor.matmul(bank(4), vt[:, b, :], p_, start=(kb == 0), stop=(kb == nkb - 1)),
                                     reads=[bvt, bp], writes=[b_ps[4]], inc=False)
                                k.op("pe", lambda: nc.tensor.matmul(bank(5), ones_b, p_, start=(kb == 0), stop=(kb == nkb - 1)),
                                     reads=[b_const, bp], writes=[b_ps[5]])
                            pend = pv
                    pend()
                    k.op("dve", lambda: nc.vector.reciprocal(out=rs, in_=bank(5)), reads=[b_ps[5]], writes=[b_rs])
                    o_, bo = oo.next()
                    k.op("dve", lambda o_=o_: nc.vector.tensor_tensor(out=o_, in0=bank(4), in1=rs, op=ALU.mult),
                         reads=[b_ps[4], b_rs], writes=[bo])
                    k.dma("sp", OT_v[:, 4 * g:4 * g + 4, j * 128:(j + 1) * 128], o_.rearrange("p (h t) -> p h t", h=4),
                          reads=[bo], writes=[b_OT])
            k.barrier()

    def phase_C():
        with ExitStack() as es:
            yain = A(es, "yain", [128, 16, 1024], BF16)
            ot = A(es, "ot", [128, 16, 1024], BF16)
            b_yain, b_ot = Buf("yain"), Buf("ot")
            for c0 in range(0, 16, 4):
                k.dma("sp", yain[:, c0:c0 + 4, :], YAIN_d[c0 * 128:(c0 + 4) * 128, :].rearrange("(c p) n -> p c n", p=128),
                      reads=[b_YAIN], writes=[b_yain])
                k.dma("sp", ot[:, c0:c0 + 4, :], OT_d[c0 * 128:(c0 + 4) * 128, :].rearrange("(c p) n -> p c n", p=128),
                      reads=[b_OT], writes=[b_ot])
            wa = mkrot(es, "wca", 2, [128, 16, 256], BF16)
            wb = mkrot(es, "wba", 2, [128, 16, 256], BF16)
            sga = mkrot(es, "sga", 2, [128, 1024], F32)
            sgb = mkrot(es, "sgb", 2, [128, 1024], F32)
            mo = mkrot(es, "mo", 2, [128, 1024], BF16)
            prs = Rot([(0, 2), (4, 6)])
            for t in range(16):
                wa3, bwa = wa.next()
                wb3, bwb = wb.next()
                k.dma("pool", wa3, wview(w_ca, 0, 2048, t * 256, 256), writes=[bwa])
                k.dma("pool", wb3, wview(w_ba, 0, 2048, t * 256, 256), writes=[bwb])
                for cch in range(2):
                    oc = t * 2 + cch
                    pa, pb = prs.next()
                    sa, bsa = sga.next()
                    sb_, bsb = sgb.next()
                    k.dma("sp", sa, SGA_d[oc * 128:(oc + 1) * 128, :], reads=[b_SGA], writes=[bsa])
                    k.dma("sp", sb_, SGB_d[oc * 128:(oc + 1) * 128, :], reads=[b_SGB], writes=[bsb])
                    proj(wa3, bwa, 16, cch * 128, lambda kc, o, n: yain[:, kc, o:o + n], [b_yain], NT_OWN, (pa, pa + 1))
                    proj(wb3, bwb, 16, cch * 128, lambda kc, o, n: ot[:, kc, o:o + n], [b_ot], NT_OWN, (pb, pb + 1))
                    k.op("dve", lambda sa=sa, pa=pa: nc.vector.tensor_tensor(out=sa, in0=pair(pa), in1=sa, op=ALU.mult),
                         reads=[b_ps[pa], b_ps[pa + 1], bsa], writes=[bsa])
                    k.op("dve", lambda sb_=sb_, pb=pb: nc.vector.tensor_tensor(out=sb_, in0=pair(pb), in1=sb_, op=ALU.mult),
                         reads=[b_ps[pb], b_ps[pb + 1], bsb], writes=[bsb])
                    m_, bm = mo.next()
                    k.op("dve", lambda m_=m_, sa=sa, sb_=sb_: nc.vector.tensor_tensor(out=m_, in0=sa, in1=sb_, op=ALU.add),
                         reads=[bsa, bsb], writes=[bm])
                    k.dma("sp", M_d[oc * 128:(oc + 1) * 128, :], m_, reads=[bm], writes=[b_M])
            k.barrier()

    def resid_proj(es, src3, b_src, nk, wd, xin_fn, xin_buf, xout_d, b_xo, wshape_cols=256):
        wr = mkrot(es, "wrp", 2, [128, nk, wshape_cols], BF16)
        xr = mkrot(es, "xrp", 3, [128, 1024], F32)
        prs = Rot([0, 2, 4, 6])
        for t in range(D // wshape_cols):
            w3, bw = wr.next()
            k.dma("pool", w3, wview(wd, 0, nk * 128, t * wshape_cols, wshape_cols), writes=[bw])
            for cch in range(wshape_cols // 128):
                oc = t * (wshape_cols // 128) + cch
                pb = prs.next()
                x_, bx = xr.next()
                k.dma("sp", x_, xin_fn(oc), reads=[xin_buf] if xin_buf is not None else [], writes=[bx])
                proj(w3, bw, nk, cch * 128, lambda kc, o, n: src3[:, kc, o:o + n], [b_src], NT_OWN, (pb, pb + 1))
                k.op("dve", lambda x_=x_, pb=pb: nc.vector.tensor_tensor(out=x_, in0=pair(pb), in1=x_, op=ALU.add),
                     reads=[b_ps[pb], b_ps[pb + 1], bx], writes=[bx])
                k.dma("sp", xout_d[oc * 128:(oc + 1) * 128, :], x_, reads=[bx], writes=[b_xo])

    def phase_D():
        with ExitStack() as es:
            mg = A(es, "mg", [128, 32, 1024], BF16)
            b_mg = Buf("mg")
            for c0 in range(0, 32, 4):
                k.dma("sp", mg[:, c0:c0 + 4, :], M_d[c0 * 128:(c0 + 4) * 128, :].rearrange("(c p) n -> p c n", p=128),
                      reads=[b_M], writes=[b_mg])
            resid_proj(es, mg, b_mg, 32, w_mo, lambda oc: xo[oc * 128:(oc + 1) * 128, 0:1024], None, X1_d, b_X1)
            k.barrier()

    def phase_E():
        with ExitStack() as es:
            hx = A(es, "hx", [128, 32, 1024], BF16)
            hm = A(es, "hm", [128, 32, 256], BF16)
            b_hx, b_hm = Buf("hx"), Buf("hm")
            qx = A(es, "qx", [128, 4, 1024], BF16)
            kx = A(es, "kx", [128, 4, 256], BF16)
            vx = A(es, "vx", [128, 2, 512], BF16)
            ox = A(es, "ox", [128, 4, 1024], BF16)
            b_qx, b_kx, b_vx, b_ox = Buf("qx"), Buf("kx"), Buf("vx"), Buf("ox")
            with ExitStack() as es2:
                res = dict(xc=mkrot(es2, "xce", 3, [128, 1024], F32), sq=mkrot(es2, "sqe", 2, [128, 1024], F32),
                           rstd=(A(es2, "rstde", [128, 1024], F32), Buf("rstde")))
                rmsnorm_T(X1_d, 1024, 1, res, (5, 6), dst3=hx, dst_buf=b_hx, src_buf=b_X1)
                rmsnorm_T(memT, 256, 2, res, (7,), dst3=hm, dst_buf=b_hm)
                k.barrier()
            wr = mkrot(es, "wxe", 2, [128, 32, 256], BF16)
            prs = Rot([0, 2])
            for t in range(2):
                w3, bw = wr.next()
                k.dma("pool", w3, wview(w_xq, 0, D, t * 256, 256), writes=[bw])
                for cch in range(2):
                    pb = prs.next()
                    proj(w3, bw, 32, cch * 128, lambda kc, o, n: hx[:, kc, o:o + n], [b_hx], NT_OWN, (pb, pb + 1))
                    copy_evac(alt(), qx[:, t * 2 + cch, :], pair(pb), [b_ps[pb], b_ps[pb + 1]], [b_qx])
            for t in range(2):
                w3, bw = wr.next()
                k.dma("pool", w3, wview(w_xkv, 0, D, t * 256, 256), writes=[bw])
                for cch in range(2):
                    pb = prs.next()
                    proj(w3, bw, 32, cch * 128, lambda kc, o, n: hm[:, kc, o:o + n], [b_hm], [(0, 256)], (pb,))
                    copy_evac(alt(), kx[:, t * 2 + cch, :], bank(pb, 256), [b_ps[pb]], [b_kx])
            for t in range(2):
                w3, bw = wr.next()
                k.dma("pool", w3, wview(w_xkv, 0, D, 512 + t * 256, 256), writes=[bw])
                for mc in range(2):
                    pb = prs.next()
                    for kc in range(32):
                        k.op("pe", lambda pb=pb, kc=kc, mc=mc, w3=w3: nc.tensor.matmul(
                            bank(pb, 256), hm[:, kc, mc * 128:(mc + 1) * 128], w3[:, kc, :], start=(kc == 0), stop=(kc == 31)),
                            reads=[bw, b_hm], writes=[b_ps[pb]], inc=(kc == 31))
                    copy_evac(alt(), vx[:, mc, t * 256:(t + 1) * 256], bank(pb, 256), [b_ps[pb]], [b_vx])
            pp = mkrot(es, "ppe", 4, [128, 512], BF16)
            rs = A(es, "rse", [128, 512], F32)
            b_rs = Buf("rse")
            lrot = Rot([0, 1, 2, 3])
            for h in range(4):
                for tt in range(2):
                    ps_ = []
                    for mc in range(2):
                        pl = lrot.next()
                        k.op("pe", lambda pl=pl, h=h, mc=mc, tt=tt: nc.tensor.matmul(
                            bank(pl), kx[:, h, mc * 128:(mc + 1) * 128], qx[:, h, tt * 512:(tt + 1) * 512], start=True, stop=True),
                            reads=[b_kx, b_qx], writes=[b_ps[pl]])
                        p_, bp = pp.next()
                        k.op("act", lambda pl=pl, p_=p_: nc.scalar.activation(out=p_, in_=bank(pl), func=AF.Exp, scale=ATT_SCALE),
                             reads=[b_ps[pl]], writes=[bp])
                        ps_.append((p_, bp))
                    for mc in range(2):
                        p_, bp = ps_[mc]
                        k.op("pe", lambda p_=p_, mc=mc, h=h: nc.tensor.matmul(
                            bank(4), vx[:, mc, h * 128:(h + 1) * 128], p_, start=(mc == 0), stop=(mc == 1)),
                            reads=[b_vx, bp], writes=[b_ps[4]], inc=False)
                        k.op("pe", lambda p_=p_, mc=mc: nc.tensor.matmul(bank(5), ones_b, p_, start=(mc == 0), stop=(mc == 1)),
                             reads=[b_const, bp], writes=[b_ps[5]])
                    k.op("dve", lambda: nc.vector.reciprocal(out=rs, in_=bank(5)), reads=[b_ps[5]], writes=[b_rs])
                    k.op("dve", lambda h=h, tt=tt: nc.vector.tensor_tensor(
                        out=ox[:, h, tt * 512:(tt + 1) * 512], in0=bank(4), in1=rs, op=ALU.mult),
                        reads=[b_ps[4], b_rs], writes=[b_ox])
            resid_proj(es, ox, b_ox, 4, w_xo, lambda oc: X1_d[oc * 128:(oc + 1) * 128, :], b_X1, X2_d, b_X2)
            k.barrier()

    def phase_F():
        with ExitStack() as es:
            h2 = A(es, "h2", [128, 32, 1024], BF16)
            b_h2 = Buf("h2")
            with ExitStack() as es2:
                res = dict(xc=mkrot(es2, "xcf", 3, [128, 1024], F32), sq=mkrot(es2, "sqf", 2, [128, 1024], F32),
                           rstd=(A(es2, "rstdf", [128, 1024], F32), Buf("rstdf")))
                rmsnorm_T(X2_d, 1024, 3, res, (6, 7), dst3=h2, dst_buf=b_h2, src_buf=b_X2)
                k.barrier()
            wr = mkrot(es, "wup", 3, [128, 32, 256], BF16)
            rl = mkrot(es, "rl", 2, [128, 1024], F32)
            ao = mkrot(es, "ao", 3, [128, 1024], BF16)
            prs = Rot([0, 2, 4])
            for t in range(DFF // 256):
                w3, bw = wr.next()
                k.dma("pool", w3, wview(w_up, 0, D, t * 256, 256), writes=[bw])
                for cch in range(2):
                    fc = t * 2 + cch
                    pb = prs.next()
                    proj(w3, bw, 32, cch * 128, lambda kc, o, n: h2[:, kc, o:o + n], [b_h2], NT_OWN, (pb, pb + 1))
                    r_, br = rl.next()
                    k.op("act", lambda r_=r_, pb=pb: nc.scalar.activation(out=r_, in_=pair(pb), func=AF.Relu),
                         reads=[b_ps[pb], b_ps[pb + 1]], writes=[br])
                    a_, ba = ao.next()
                    k.op("dve", lambda r_=r_, a_=a_: nc.vector.tensor_tensor(out=a_, in0=r_, in1=r_, op=ALU.mult),
                         reads=[br], writes=[ba])
                    k.dma("sp", A_d[fc * 128:(fc + 1) * 128, :], a_, reads=[ba], writes=[b_A])
            k.barrier()
        with ExitStack() as es:
            ah = A(es, "ah", [128, 64, 1024], BF16)
            b_ah = Buf("ah")
            wr = mkrot(es, "wdn", 2, [128, 64, 128], BF16)
            xr = mkrot(es, "xrf", 2, [128, 1024], F32)
            pr = mkrot(es, "prf", 2, [128, 1024], F32)
            prs = Rot([0, 2, 4, 6])
            for kh in range(2):
                for c0 in range(0, 64, 8):
                    r0 = kh * 8192 + c0 * 128
                    k.dma("sp", ah[:, c0:c0 + 8, :], A_d[r0:r0 + 1024, :].rearrange("(c p) n -> p c n", p=128),
                          reads=[b_A], writes=[b_ah])
                for oc in range(32):
                    w3, bw = wr.next()
                    k.dma("pool", w3, wview(w_dn, kh * 8192, 8192, oc * 128, 128), writes=[bw])
                    pb = prs.next()
                    proj(w3, bw, 64, 0, lambda kc, o, n: ah[:, kc, o:o + n], [b_ah], NT_OWN, (pb, pb + 1))
                    p_, bp = pr.next()
                    if kh == 0:
                        copy_evac(alt(), p_, pair(pb), [b_ps[pb], b_ps[pb + 1]], [bp])
                        k.dma("sp", P_d[oc * 128:(oc + 1) * 128, :], p_, reads=[bp], writes=[b_P])
                    else:
                        x_, bx = xr.next()
                        k.dma("sp", p_, P_d[oc * 128:(oc + 1) * 128, :], reads=[b_P], writes=[bp])
                        k.dma("sp", x_, X2_d[oc * 128:(oc + 1) * 128, :], reads=[b_X2], writes=[bx])
                        k.op("dve", lambda p_=p_, pb=pb: nc.vector.tensor_tensor(out=p_, in0=pair(pb), in1=p_, op=ALU.add),
                             reads=[b_ps[pb], b_ps[pb + 1], bp], writes=[bp])
                        k.op("dve", lambda p_=p_, x_=x_: nc.vector.tensor_tensor(out=x_, in0=p_, in1=x_, op=ALU.add),
                             reads=[bp, bx], writes=[bx])
                        k.dma("sp", xout[oc * 128:(oc + 1) * 128, :], x_, reads=[bx], writes=[b_xout])
            k.barrier()

    def phase_G():
        with ExitStack() as es:
            res = dict(xc=mkrot(es, "xcg", 3, [128, 1024], F32), sq=mkrot(es, "sqg", 2, [128, 1024], F32),
                       rstd=(A(es, "rstdg", [128, 1024], F32), Buf("rstdg")), of=mkrot(es, "ofg", 3, [128, 1024], F32))
            rmsnorm_T(xout, 1024, 4, res, (6, 7), src_buf=b_xout, out_d=yn, out_buf=b_yn)
            k.barrier()

    phases = dict(K=phase_K, P=phase_P, A=phase_A, C=phase_C, D=phase_D, E=phase_E, F=phase_F, G=phase_G)
    for ph in PHASES:
        phases[ph]()
    k.barrier()
    return nc, k


_PROG = {}


def _get_prog(phases="KPACDEFG", debug_out=()):
    key = (phases, tuple(debug_out))
    if key not in _PROG:
        _PROG[key] = build_program(phases, debug_out)
    return _PROG[key]


def _t5_bucket_np(dist):
    n = np.maximum(dist, 0)
    nf = np.maximum(n, 1).astype(np.float32)
    large = 16 + (np.log(nf / np.float32(16)) / np.float32(math.log(128 / 16)) * np.float32(16)).astype(np.int32)
    large = np.minimum(large, 31)
    return np.where(n < 16, n, large)


def _own_idx(c):
    return np.concatenate([np.arange((8 * j + c) * 128, (8 * j + c + 1) * 128) for j in range(8)])


def _consts(rel_bias):
    rel_bias = np.asarray(rel_bias, np.float32)
    s = np.arange(128)[:, None]
    t = np.arange(128)[None, :]
    biasn = np.zeros((128, 2, 16, 128), np.float32)
    for off in range(2):
        bk = _t5_bucket_np(t - s + 128 * off)
        biasn[:, off, :, :] = rel_bias[bk].transpose(0, 2, 1)
    bfar = np.ascontiguousarray(np.broadcast_to(rel_bias[31][None, :], (128, 16))).astype(np.float32)
    identrep = np.ascontiguousarray(np.tile(np.eye(128, dtype=np.float32), (1, 4)))
    tt = np.arange(128)[:, None]
    ss = np.arange(128)[None, :]
    cmask = np.where(ss > tt, np.float32(BIG), np.float32(0)).astype(np.float32)
    return dict(biasn=biasn.reshape(128, -1), bfar=bfar, identrep=identrep, cmask=cmask)


def _layer_in_maps(l, xfull, inp, consts, names):
    xT = np.ascontiguousarray(xfull.T)
    g = np.stack([inp["norm_mix"][l], inp["norm_xattn"][l], inp["norm_mem"][l], inp["norm_mlp"][l], inp["norm_final"]])
    gains = np.ascontiguousarray(g.reshape(5, 32, 128).transpose(2, 0, 1).reshape(128, 160)).astype(np.float32)
    convw = np.ascontiguousarray(inp["conv_w"][l].reshape(3, 16, 128).transpose(2, 1, 0).reshape(128, 48)).astype(np.float32)
    shared = dict(
        w_in=inp["w_in"][l], convw=convw, w_ca=inp["w_conv_out"][l], w_ba=inp["w_attn_out"][l],
        w_mo=inp["w_mix_out"][l], w_xq=inp["w_xq"][l], w_xkv=inp["w_xkv"][l], w_xo=inp["w_xo"][l],
        w_up=inp["w_up"][l], w_dn=inp["w_down"][l], gains=gains,
        memT=np.ascontiguousarray(inp["mem"][0].T), **consts)
    maps = []
    for c in range(NCORE):
        m = {}
        if "xo" in names:
            xo = np.zeros((D, NH), np.float32)
            xo[:, :NOWN] = xT[:, _own_idx(c)]
            for j in range(8):
                t0 = (8 * j + c) * 128
                if t0 >= 2:
                    xo[:, NOWN + 2 * j:NOWN + 2 * j + 2] = xT[:, t0 - 2:t0]
            m["xo"] = xo
        if "xall" in names:
            xa = np.zeros((D, S), np.float32)
            sh = (7 - c) * 128
            xa[:, sh:] = xT[:, :S - sh]
            m["xall"] = xa
        if "padmask" in names:
            pm = np.zeros((128, 1024), np.float32)
            pm[:, :(7 - c) * 128] = BIG
            m["padmask"] = pm
        for n in names:
            if n not in m:
                m[n] = np.ascontiguousarray(np.asarray(shared[n], np.float32))
        maps.append(m)
    return maps


def kernel(**inputs):
    inp = {k_: np.asarray(v) for k_, v in inputs.items()}
    nc, kb = _get_prog()
    names = kb.in_names
    consts = _consts(inp["rel_bias"])
    x = np.asarray(inp["x"][0], np.float32)
    out = None
    for l in range(2):
        maps = _layer_in_maps(l, x, inp, consts, names)
        res = run_bass_kernel_spmd(nc, maps, core_ids=list(range(NCORE)))
        del maps
        key = "xout" if l == 0 else "yn"
        nxt = np.empty((S, D), np.float32)
        for c in range(NCORE):
            nxt[_own_idx(c)] = res.results[c][key].T
        x = nxt
    return x[None].astype(np.float32)
```

```python
import math
import numpy as np
import concourse.bass as bass
import concourse.mybir as mybir
from concourse.bass_utils import run_bass_kernel_spmd

F32 = mybir.dt.float32
BF16 = mybir.dt.bfloat16
AF = mybir.ActivationFunctionType
ALU = mybir.AluOpType
AX = mybir.AxisListType

D = 4096
S = 8192
NCORE = 8
NOWN = 1024
NH = 1040
CONV = 2048
IN_DIM = 19552
OFF_U, OFF_CB, OFF_CC, OFF_Q, OFF_K, OFF_V = 0, 2048, 4096, 6144, 8192, 8704
OFF_IQ, OFF_IK, OFF_IW, OFF_GA, OFF_GB = 9216, 11264, 11328, 11360, 15456
DFF = 16384
EPS = 1e-6
TOPK = 256
NEG = -30000.0
BIG = -1.0e30
ATT_SCALE = 128 ** -0.5
IW_SCALE = (32 ** -0.5) * (64 ** -0.5)
NBIS = 14


class Buf:
    __slots__ = ("name", "w", "r")

    def __init__(self, name):
        self.name = name
        self.w = None
        self.r = {}


class KB:
    COMPUTE = ("pe", "act", "dve", "pool")
    NPOOL = 12
    LIMIT = 30000

    def __init__(self, nc):
        self.nc = nc
        self.eng = {"pe": nc.tensor, "act": nc.scalar, "dve": nc.vector, "pool": nc.gpsimd, "sp": nc.sync}
        self.nsem = 0
        self.sem = {e: self._newsem(e) for e in self.COMPUTE}
        self.cnt = {e: 0 for e in self.COMPUTE}
        self.waited = {e: {} for e in self.eng}
        self.dsem = {q: [self._newsem("d" + q) for _ in range(self.NPOOL)] for q in ("sp", "pool")}
        self.dcnt = {q: [0] * self.NPOOL for q in ("sp", "pool")}
        self.drr = {q: 0 for q in ("sp", "pool")}
        self.nins = 0

    def _newsem(self, tag):
        self.nsem += 1
        return self.nc.alloc_semaphore(f"sm_{tag}_{self.nsem}")

    def _wait(self, e, tick):
        te, sem, val = tick
        key = id(sem)
        if self.waited[e].get(key, 0) >= val:
            return
        self.eng[e].wait_ge(sem, val)
        self.waited[e][key] = val

    def _deps(self, e, reads, writes):
        for b in reads:
            if b.w is not None:
                t = b.w
                if not (t[0] == e and e == "pe"):
                    self._wait(e, t)
        for b in writes:
            if b.w is not None:
                t = b.w
                if not (t[0] == e and e == "pe"):
                    self._wait(e, t)
            for t in b.r.values():
                if t[0] == e:
                    continue
                self._wait(e, t)

    def _mark(self, tick, reads, writes):
        for b in writes:
            b.w = tick
            b.r = {}
        for b in reads:
            b.r[(tick[0], id(tick[1]))] = tick

    def op(self, e, fn, reads=(), writes=(), inc=True):
        self._deps(e, reads, writes)
        ins = fn()
        self.nins += 1
        if inc:
            if self.cnt[e] >= self.LIMIT:
                self.sem[e] = self._newsem(e)
                self.cnt[e] = 0
            self.cnt[e] += 1
            ins.then_inc(self.sem[e], 1)
            tick = (e, self.sem[e], self.cnt[e])
        else:
            tick = (e, self.sem[e], self.cnt[e] + 1)
        self._mark(tick, reads, writes)
        return tick

    def dma(self, q, out, in_, reads=(), writes=()):
        if isinstance(in_, LazyIn):
            in_ = in_.ap()
        self._deps(q, reads, writes)
        i = self.drr[q] % self.NPOOL
        self.drr[q] += 1
        sem = self.dsem[q][i]
        if self.dcnt[q][i] > 0:
            self._wait(q, ("dma", sem, 16 * self.dcnt[q][i]))
        self.eng[q].dma_start(out=out, in_=in_).then_inc(sem, 16)
        self.nins += 1
        self.dcnt[q][i] += 1
        tick = ("dma", sem, 16 * self.dcnt[q][i])
        self._mark(tick, reads, writes)
        return tick

    def all_ticks(self):
        ts = []
        for e in self.COMPUTE:
            if self.cnt[e] > 0:
                ts.append((e, self.sem[e], self.cnt[e]))
        for q in self.dsem:
            for i, sem in enumerate(self.dsem[q]):
                if self.dcnt[q][i] > 0:
                    ts.append(("dma", sem, 16 * self.dcnt[q][i]))
        return ts

    def barrier(self, engines=None):
        ts = self.all_ticks()
        for e in (engines or self.eng):
            for t in ts:
                self._wait(e, t)


class LazyIn:
    def __init__(self, nc, name, shape, reg):
        self.nc, self.name, self.shape, self.reg, self._ap = nc, name, shape, reg, None

    def ap(self):
        if self._ap is None:
            self._ap = self.nc.dram_tensor(self.name, list(self.shape), F32, kind="ExternalInput").ap()
            self.reg.append(self.name)
        return self._ap

    def __getitem__(self, idx):
        return self.ap()[idx]

    def rearrange(self, *a, **kw):
        return self.ap().rearrange(*a, **kw)


class Rot:
    def __init__(self, items):
        self.items = items
        self.i = 0

    def next(self):
        it = self.items[self.i % len(self.items)]
        self.i += 1
        return it


def build_program(PHASES="KPACDEFG", DEBUG_OUT=()):
    nc = bass.Bass("TRN2", target_bir_lowering=False)
    k = KB(nc)

    in_names = []
    k.in_names = in_names

    def din(name, shape, dt=F32):
        return LazyIn(nc, name, shape, in_names)

    def dscr(name, shape, dt):
        if name in DEBUG_OUT:
            return nc.dram_tensor(name, list(shape), dt, kind="ExternalOutput").ap()
        return nc.dram_tensor(name, list(shape), dt).ap()

    xo = din("xo", [D, NH])
    xall = din("xall", [D, S])
    padmask_d = din("padmask", [128, 1024])
    w_in = din("w_in", [D, IN_DIM])
    convw_d = din("convw", [128, 16 * 3])
    w_ca = din("w_ca", [CONV, D])
    w_ba = din("w_ba", [2048, D])
    w_mo = din("w_mo", [D, D])
    w_xq = din("w_xq", [D, 512])
    w_xkv = din("w_xkv", [D, 1024])
    w_xo = din("w_xo", [512, D])
    w_up = din("w_up", [D, DFF])
    w_dn = din("w_dn", [DFF, D])
    gains_d = din("gains", [128, 5 * 32])
    memT = din("memT", [D, 256])
    biasn_d = din("biasn", [128, 2 * 16 * 128])
    bfar_d = din("bfar", [128, 16])
    identrep_d = din("identrep", [128, 512])
    cmask_d = din("cmask", [128, 128])
    xout = nc.dram_tensor("xout", [D, NOWN], F32, kind="ExternalOutput").ap()
    yn = nc.dram_tensor("yn", [D, NOWN], F32, kind="ExternalOutput").ap()

    KT_d = dscr("KT_d", [4, 128, S], BF16)
    V_d = dscr("V_d", [S, 512], BF16)
    IK_d = dscr("IK_d", [64, S], BF16)
    QT_d = dscr("QT_d", [2048, NOWN], BF16)
    IQT_d = dscr("IQT_d", [2048, NOWN], BF16)
    YAIN_d = dscr("YAIN_d", [2048, NOWN], BF16)
    OT_d = dscr("OT_d", [2048, NOWN], BF16)
    SGA_d = dscr("SGA_d", [D, NOWN], F32)
    SGB_d = dscr("SGB_d", [D, NOWN], F32)
    M_d = dscr("M_d", [D, NOWN], BF16)
    X1_d = dscr("X1_d", [D, NOWN], F32)
    X2_d = dscr("X2_d", [D, NOWN], F32)
    A_d = dscr("A_d", [DFF, NOWN], BF16)
    P_d = dscr("P_d", [D, NOWN], F32)
    b_KT, b_V, b_IK = Buf("KT_d"), Buf("V_d"), Buf("IK_d")
    b_QT, b_IQT, b_YAIN, b_OT = Buf("QT_d"), Buf("IQT_d"), Buf("YAIN_d"), Buf("OT_d")
    b_SGA, b_SGB, b_M, b_X1, b_X2, b_A, b_P = (Buf(n) for n in ("SGA", "SGB", "M", "X1", "X2", "A", "P"))
    b_xout = Buf("xout")
    b_yn = Buf("yn")

    psum = nc.alloc_psum_tensor("psum", [128, 8 * 512], F32).ap()

    def bank(i, n=512):
        return psum[:, i * 512:i * 512 + n]

    b_ps = [Buf(f"ps{i}") for i in range(8)]

    def sb(name, shape, dt):
        return nc.alloc_sbuf_tensor("sb_" + name, list(shape), dt).ap()

    gains = sb("gains", [128, 5 * 32], F32)
    convw = sb("convw", [128, 48], F32)
    ones_f = sb("ones_f", [128, 128], F32)
    ones_b = sb("ones_b", [128, 128], BF16)
    iw_sb = sb("iw_sb", [128, 8 * 32], F32)
    b_const = Buf("const")
    b_iw = Buf("iw")
    k.dma("sp", gains, gains_d, writes=[b_const])
    k.dma("sp", convw, convw_d, writes=[b_const])
    k.op("dve", lambda: nc.vector.memset(ones_f, 1.0), writes=[b_const])
    k.op("dve", lambda: nc.vector.memset(ones_b, 1.0), writes=[b_const])

    from contextlib import ExitStack

    uid = [0]

    def A(es, name, shape, dt):
        uid[0] += 1
        t = es.enter_context(nc.sbuf_tensor(f"t{uid[0]}_" + name, list(shape), dt))
        return t.ap() if hasattr(t, "ap") else t[:]

    def mkrot(es, name, n, shape, dt):
        return Rot([(A(es, f"{name}{i}", shape, dt), Buf(f"{name}{i}")) for i in range(n)])

    flip = [0]

    def alt():
        flip[0] ^= 1
        return "act" if flip[0] else "dve"

    def copy_evac(e, out, in_, reads, writes):
        if e == "act":
            return k.op("act", lambda: nc.scalar.copy(out=out, in_=in_), reads=reads, writes=writes)
        return k.op("dve", lambda: nc.vector.tensor_copy(out=out, in_=in_), reads=reads, writes=writes)

    def rmsnorm_T(src, ncols, gidx, res, ps_banks, dst3=None, dst_buf=None, src_buf=None, out_d=None, out_buf=None):
        ntiles = [(o, min(512, ncols - o)) for o in range(0, ncols, 512)]
        rd = [src_buf] if src_buf is not None else []
        rstd, b_rstd = res["rstd"]
        for c in range(32):
            xc, bx = res["xc"].next()
            k.dma("sp", xc[:, :ncols], src[c * 128:(c + 1) * 128, :], reads=rd, writes=[bx])
            sq, bq = res["sq"].next()
            k.op("act", lambda xc=xc, sq=sq: nc.scalar.activation(out=sq[:, :ncols], in_=xc[:, :ncols], func=AF.Square),
                 reads=[bx], writes=[bq])
            for ti, (o, n) in enumerate(ntiles):
                pb = ps_banks[ti]
                k.op("pe", lambda pb=pb, sq=sq, o=o, n=n, c=c: nc.tensor.matmul(
                    bank(pb, n), ones_f, sq[:, o:o + n], start=(c == 0), stop=(c == 31)),
                    reads=[bq, b_const], writes=[b_ps[pb]], inc=(c == 31 or ti == len(ntiles) - 1))
        for ti, (o, n) in enumerate(ntiles):
            pb = ps_banks[ti]
            k.op("dve", lambda pb=pb, o=o, n=n: nc.vector.tensor_scalar(
                out=rstd[:, o:o + n], in0=bank(pb, n), scalar1=1.0 / D, scalar2=EPS, op0=ALU.mult, op1=ALU.add),
                reads=[b_ps[pb]], writes=[b_rstd])
        k.op("act", lambda: nc.scalar.activation(out=rstd[:, :ncols], in_=rstd[:, :ncols], func=AF.Sqrt),
             reads=[b_rstd], writes=[b_rstd])
        k.op("dve", lambda: nc.vector.reciprocal(out=rstd[:, :ncols], in_=rstd[:, :ncols]),
             reads=[b_rstd], writes=[b_rstd])
        for c in range(32):
            xc, bx = res["xc"].next()
            k.dma("sp", xc[:, :ncols], src[c * 128:(c + 1) * 128, :], reads=rd, writes=[bx])
            gs = gains[:, gidx * 32 + c:gidx * 32 + c + 1]
            if out_d is None:
                k.op("dve", lambda xc=xc, c=c, gs=gs: nc.vector.scalar_tensor_tensor(
                    out=dst3[:, c, :ncols], in0=xc[:, :ncols], scalar=gs,
                    in1=rstd[:, :ncols], op0=ALU.mult, op1=ALU.mult),
                    reads=[bx, b_rstd, b_const], writes=[dst_buf])
            else:
                of, bo = res["of"].next()
                k.op("dve", lambda xc=xc, of=of, gs=gs: nc.vector.scalar_tensor_tensor(
                    out=of[:, :ncols], in0=xc[:, :ncols], scalar=gs,
                    in1=rstd[:, :ncols], op0=ALU.mult, op1=ALU.mult),
                    reads=[bx, b_rstd, b_const], writes=[bo])
                k.dma("sp", out_d[c * 128:(c + 1) * 128, :], of[:, :ncols], reads=[bo], writes=[out_buf])

    def wview(wd, r0, nrows, c0, ncols):
        return wd[r0:r0 + nrows, c0:c0 + ncols].rearrange("(c p) n -> p c n", p=128)

    def proj(w3, bw, nk, col0, rhs_fn, rhs_bufs, ntiles, pbanks, m=128):
        for kc in range(nk):
            for ti, (o, n) in enumerate(ntiles):
                pb = pbanks[ti]
                k.op("pe", lambda pb=pb, kc=kc, o=o, n=n: nc.tensor.matmul(
                    psum[0:m, pb * 512:pb * 512 + n], w3[:, kc, col0:col0 + m], rhs_fn(kc, o, n),
                    start=(kc == 0), stop=(kc == nk - 1)),
                    reads=[bw] + rhs_bufs, writes=[b_ps[pb]], inc=(kc == nk - 1))

    NT_OWN = [(0, 512), (512, 512)]
    NT_HALO = [(0, 512), (512, 512), (1024, 16)]

    def pair(pb):
        return psum[:, pb * 512:pb * 512 + 1024]

    def phase_K():
        with ExitStack() as es:
            wk = A(es, "wk", [128, 32, 512], BF16)
            wv = A(es, "wv", [128, 32, 512], BF16)
            wik = A(es, "wik", [128, 32, 64], BF16)
            bwk, bwv, bwik = Buf("wk"), Buf("wv"), Buf("wik")
            for c0 in (0, 256):
                k.dma("pool", wk[:, :, c0:c0 + 256], wview(w_in, 0, D, OFF_K + c0, 256), writes=[bwk])
            for c0 in (0, 256):
                k.dma("pool", wv[:, :, c0:c0 + 256], wview(w_in, 0, D, OFF_V + c0, 256), writes=[bwv])
            k.dma("pool", wik, wview(w_in, 0, D, OFF_IK, 64), writes=[bwik])
            hts = mkrot(es, "htk", 2, [128, 32, 512], BF16)
            res = dict(xc=mkrot(es, "xck", 3, [128, 512], F32), sq=mkrot(es, "sqk", 2, [128, 512], F32),
                       rstd=(A(es, "rstdk", [128, 512], F32), Buf("rstdk")))
            kto = mkrot(es, "kto", 2, [128, 4, 512], BF16)
            iko = mkrot(es, "iko", 2, [64, 512], BF16)
            vo = mkrot(es, "vo", 2, [128, 4, 512], BF16)
            KT_v = KT_d.rearrange("g p n -> p g n")
            V_v = V_d.rearrange("(b p) n -> p b n", p=128)
            prot = Rot([0, 1, 2, 3, 4, 5])
            for tt in range(S // 512):
                h3, bh = hts.next()
                rmsnorm_T(xall[:, tt * 512:(tt + 1) * 512], 512, 0, res, (7,), dst3=h3, dst_buf=bh)
                kt, bkt = kto.next()
                for g in range(4):
                    pb = prot.next()
                    proj(wk, bwk, 32, g * 128, lambda kc, o, n, h3=h3: h3[:, kc, o:o + n], [bh], [(0, 512)], (pb,))
                    copy_evac(alt(), kt[:, g, :], bank(pb), [b_ps[pb]], [bkt])
                k.dma("sp", KT_v[:, :, tt * 512:(tt + 1) * 512], kt, reads=[bkt], writes=[b_KT])
                ikt, bikt = iko.next()
                pb = prot.next()
                proj(wik, bwik, 32, 0, lambda kc, o, n, h3=h3: h3[:, kc, o:o + n], [bh], [(0, 512)], (pb,), m=64)
                copy_evac(alt(), ikt, psum[0:64, pb * 512:(pb + 1) * 512], [b_ps[pb]], [bikt])
                k.dma("sp", IK_d[:, tt * 512:(tt + 1) * 512], ikt, reads=[bikt], writes=[b_IK])
                vt, bvt = vo.next()
                for s4 in range(4):
                    pb = prot.next()
                    for kc in range(32):
                        k.op("pe", lambda pb=pb, kc=kc, s4=s4, h3=h3: nc.tensor.matmul(
                            bank(pb), h3[:, kc, s4 * 128:(s4 + 1) * 128], wv[:, kc, :], start=(kc == 0), stop=(kc == 31)),
                            reads=[bwv, bh], writes=[b_ps[pb]], inc=(kc == 31))
                    copy_evac(alt(), vt[:, s4, :], bank(pb), [b_ps[pb]], [bvt])
                k.dma("sp", V_v[:, tt * 4:(tt + 1) * 4, :], vt, reads=[bvt], writes=[b_V])
            k.barrier()

    def phase_P():
        with ExitStack() as es:
            h3 = A(es, "hTo", [128, 32, NH], BF16)
            bh = Buf("hTo")
            with ExitStack() as es2:
                res = dict(xc=mkrot(es2, "xcp", 3, [128, NH], F32), sq=mkrot(es2, "sqp", 2, [128, NH], F32),
                           rstd=(A(es2, "rstdp", [128, NH], F32), Buf("rstdp")))
                rmsnorm_T(xo, NH, 0, res, (5, 6, 7), dst3=h3, dst_buf=bh)
                k.barrier()
            wrot = mkrot(es, "wp", 2, [128, 32, 384], BF16)
            rhs_h = lambda kc, o, n: h3[:, kc, o:o + n]
            uf = A(es, "uf", [128, NH], F32)
            cu = A(es, "cu", [128, 8, 130], F32)
            cv = A(es, "cv", [128, 8, 128], F32)
            b_uf, b_cu, b_cv = Buf("uf"), Buf("cu"), Buf("cv")
            yo = mkrot(es, "yo", 2, [128, 1024], BF16)
            for i in range(16):
                w3, bw = wrot.next()
                for si, off in enumerate((OFF_U, OFF_CB, OFF_CC)):
                    k.dma("pool", w3[:, :, si * 128:(si + 1) * 128], wview(w_in, 0, D, off + i * 128, 128), writes=[bw])
                proj(w3, bw, 32, 0, rhs_h, [bh], NT_HALO, (0, 1, 4))
                k.op("act", lambda: nc.scalar.copy(out=uf[:, 0:1024], in_=pair(0)), reads=[b_ps[0], b_ps[1]], writes=[b_uf])
                k.op("act", lambda: nc.scalar.copy(out=uf[:, 1024:1040], in_=bank(4, 16)), reads=[b_ps[4]], writes=[b_uf])
                proj(w3, bw, 32, 256, rhs_h, [bh], NT_HALO, (2, 3, 5))
                k.op("dve", lambda: nc.vector.tensor_tensor(
                    out=cu[:, :, 2:130], in0=pair(2).rearrange("p (j t) -> p j t", j=8),
                    in1=uf[:, 0:1024].rearrange("p (j t) -> p j t", j=8), op=ALU.mult),
                    reads=[b_ps[2], b_ps[3], b_uf], writes=[b_cu])
                k.op("dve", lambda: nc.vector.tensor_tensor(
                    out=cu[:, :, 0:2], in0=bank(5, 16).rearrange("p (j t) -> p j t", j=8),
                    in1=uf[:, 1024:1040].rearrange("p (j t) -> p j t", j=8), op=ALU.mult),
                    reads=[b_ps[5], b_uf], writes=[b_cu])
                k.op("dve", lambda i=i: nc.vector.tensor_scalar(
                    out=cv, in0=cu[:, :, 2:130], scalar1=convw[:, i * 3 + 2:i * 3 + 3], scalar2=None, op0=ALU.mult),
                    reads=[b_cu, b_const], writes=[b_cv])
                k.op("dve", lambda i=i: nc.vector.scalar_tensor_tensor(
                    out=cv, in0=cu[:, :, 1:129], scalar=convw[:, i * 3 + 1:i * 3 + 2], in1=cv, op0=ALU.mult, op1=ALU.add),
                    reads=[b_cu, b_cv, b_const], writes=[b_cv])
                k.op("dve", lambda i=i: nc.vector.scalar_tensor_tensor(
                    out=cv, in0=cu[:, :, 0:128], scalar=convw[:, i * 3:i * 3 + 1], in1=cv, op0=ALU.mult, op1=ALU.add),
                    reads=[b_cu, b_cv, b_const], writes=[b_cv])
                proj(w3, bw, 32, 128, rhs_h, [bh], NT_OWN, (6, 7))
                y, by = yo.next()
                k.op("dve", lambda y=y: nc.vector.tensor_tensor(
                    out=y, in0=pair(6), in1=cv.rearrange("p j t -> p (j t)"), op=ALU.mult),
                    reads=[b_ps[6], b_ps[7], b_cv], writes=[by])
                k.dma("sp", YAIN_d[i * 128:(i + 1) * 128, :], y, reads=[by], writes=[b_YAIN])
            ob = mkrot(es, "obp", 3, [128, 1024], BF16)
            prs = Rot([0, 2, 4, 6])
            for off, dst, bdst in ((OFF_Q, QT_d, b_QT), (OFF_IQ, IQT_d, b_IQT)):
                for t in range(8):
                    w3, bw = wrot.next()
                    k.dma("pool", w3[:, :, 0:256], wview(w_in, 0, D, off + t * 256, 256), writes=[bw])
                    for cch in range(2):
                        pb = prs.next()
                        proj(w3, bw, 32, cch * 128, rhs_h, [bh], NT_OWN, (pb, pb + 1))
                        o_, bo = ob.next()
                        copy_evac(alt(), o_, pair(pb), [b_ps[pb], b_ps[pb + 1]], [bo])
                        r0 = (t * 2 + cch) * 128
                        k.dma("sp", dst[r0:r0 + 128, :], o_, reads=[bo], writes=[bdst])
            w3, bw = wrot.next()
            k.dma("pool", w3[:, :, 0:32], wview(w_in, 0, D, OFF_IW, 32), writes=[bw])
            pb = prs.next()
            for j in range(8):
                for kc in range(32):
                    k.op("pe", lambda j=j, kc=kc, pb=pb, w3=w3: nc.tensor.matmul(
                        psum[:, pb * 512 + j * 32:pb * 512 + (j + 1) * 32], h3[:, kc, j * 128:(j + 1) * 128],
                        w3[:, kc, 0:32], start=(kc == 0), stop=(kc == 31)),
                        reads=[bw, bh], writes=[b_ps[pb]], inc=(kc == 31))
            k.op("dve", lambda pb=pb: nc.vector.tensor_scalar(
                out=iw_sb, in0=bank(pb, 256), scalar1=IW_SCALE, scalar2=None, op0=ALU.mult),
                reads=[b_ps[pb]], writes=[b_iw])
            og = mkrot(es, "ogp", 3, [128, 1024], F32)
            for off, dst, bdst in ((OFF_GA, SGA_d, b_SGA), (OFF_GB, SGB_d, b_SGB)):
                for t in range(16):
                    w3, bw = wrot.next()
                    k.dma("pool", w3[:, :, 0:256], wview(w_in, 0, D, off + t * 256, 256), writes=[bw])
                    for cch in range(2):
                        pb = prs.next()
                        proj(w3, bw, 32, cch * 128, rhs_h, [bh], NT_OWN, (pb, pb + 1))
                        o_, bo = og.next()
                        k.op("act", lambda o_=o_, pb=pb: nc.scalar.activation(out=o_, in_=pair(pb), func=AF.Sigmoid),
                             reads=[b_ps[pb], b_ps[pb + 1]], writes=[bo])
                        r0 = (t * 2 + cch) * 128
                        k.dma("sp", dst[r0:r0 + 128, :], o_, reads=[bo], writes=[bdst])
            k.barrier()

    def phase_A():
        with ExitStack() as es:
            identrep = A(es, "identrep", [128, 512], BF16)
            cmask = A(es, "cmask", [128, 128], F32)
            padmask = A(es, "padmask", [128, 1024], F32)
            biasn = A(es, "biasn", [128, 2 * 16 * 128], F32)
            bfar = A(es, "bfar", [128, 16], F32)
            b_ac = Buf("attnconst")
            k.dma("sp", cmask, cmask_d, writes=[b_ac])
            k.dma("sp", padmask, padmask_d, writes=[b_ac])
            k.dma("sp", biasn, biasn_d, writes=[b_ac])
            k.dma("sp", bfar, bfar_d, writes=[b_ac])
            k.dma("pool", identrep, identrep_d, writes=[b_ac])
            for off in range(2):
                for h in range(16):
                    sl = biasn[:, (off * 16 + h) * 128:(off * 16 + h + 1) * 128]
                    k.op("dve", lambda sl=sl, h=h: nc.vector.tensor_scalar(
                        out=sl, in0=sl, scalar1=bfar[:, h:h + 1], scalar2=None, op0=ALU.subtract),
                        reads=[b_ac], writes=[b_ac])
            sc = A(es, "sc", [128, S], F32)
            b_sc = [Buf(f"sc{i}") for i in range(16)]
            negm = A(es, "negm", [128, S], BF16)
            b_negm = Buf("negm")
            qTs = mkrot(es, "qT", 2, [128, 2048], BF16)
            iqTs = mkrot(es, "iqT", 2, [128, 16, 128], BF16)
            ik2 = mkrot(es, "ik2", 3, [128, 512], BF16)
            rr = mkrot(es, "rr", 3, [128, 512], BF16)
            sm = A(es, "sm", [128, 16], F32)
            b_sm = Buf("sm")
            mx, mn, lo, hi, mid, cnt, ge, d1, half = (sm[:, i:i + 1] for i in range(9))
            k.op("dve", lambda: nc.vector.memset(half, 0.5), writes=[b_sm])
            ktc = mkrot(es, "ktc", 2, [128, 1024], BF16)
            vc = mkrot(es, "vc", 2, [128, 8, 128], BF16)
            pp = mkrot(es, "pp", 3, [128, 512], BF16)
            tmpf = mkrot(es, "tmpf", 2, [128, 512], F32)
            rs = A(es, "rs", [128, 512], F32)
            b_rs = Buf("rs")
            oo = mkrot(es, "oo", 2, [128, 512], BF16)
            QT_v = QT_d.rearrange("(h p) n -> p h n", p=128)
            IQT_v = IQT_d.rearrange("(h p) n -> p h n", p=128)
            OT_v = OT_d.rearrange("(h p) n -> p h n", p=128)
            V_v = V_d.rearrange("(b p) n -> p b n", p=128)
            irot = Rot([0, 1, 2, 3])
            lrot = Rot([6, 7])
            for j in range(8):
                L = 1024 * (j + 1)
                nkb = 8 * (j + 1)
                qT, bq = qTs.next()
                iqT, biq = iqTs.next()
                k.dma("sp", qT.rearrange("p (h t) -> p h t", h=16), QT_v[:, :, j * 128:(j + 1) * 128], reads=[b_QT], writes=[bq])
                k.dma("sp", iqT, IQT_v[:, :, j * 128:(j + 1) * 128], reads=[b_IQT], writes=[biq])
                for c5 in range(2 * (j + 1)):
                    ikt, bik = ik2.next()
                    k.dma("sp", ikt[0:64, :], IK_d[:, c5 * 512:(c5 + 1) * 512], reads=[b_IK], writes=[bik])
                    k.dma("sp", ikt[64:128, :], IK_d[:, c5 * 512:(c5 + 1) * 512], reads=[b_IK], writes=[bik])
                    scs = sc[:, c5 * 512:(c5 + 1) * 512]
                    for h in range(32):
                        ch, hf = h // 2, h % 2
                        pb = irot.next()
                        k.op("pe", lambda pb=pb, ch=ch, hf=hf, iqT=iqT, ikt=ikt: nc.tensor.matmul(
                            bank(pb), iqT[hf * 64:(hf + 1) * 64, ch, :], ikt[hf * 64:(hf + 1) * 64, :], start=True, stop=True),
                            reads=[biq, bik], writes=[b_ps[pb]])
                        r_, br = rr.next()
                        k.op("act", lambda pb=pb, r_=r_: nc.scalar.activation(out=r_, in_=bank(pb), func=AF.Relu),
                             reads=[b_ps[pb]], writes=[br])
                        iws = iw_sb[:, j * 32 + h:j * 32 + h + 1]
                        if h == 0:
                            k.op("dve", lambda r_=r_, iws=iws, scs=scs: nc.vector.tensor_scalar(
                                out=scs, in0=r_, scalar1=iws, scalar2=None, op0=ALU.mult),
                                reads=[br, b_iw], writes=[b_sc[c5]])
                        else:
                            k.op("dve", lambda r_=r_, iws=iws, scs=scs: nc.vector.scalar_tensor_tensor(
                                out=scs, in0=r_, scalar=iws, in1=scs, op0=ALU.mult, op1=ALU.add),
                                reads=[br, b_iw, b_sc[c5]], writes=[b_sc[c5]])
                scb = b_sc[:2 * (j + 1)]
                scL = sc[:, :L]
                k.op("dve", lambda scL=scL: nc.vector.tensor_reduce(out=mx, in_=scL, axis=AX.X, op=ALU.max), reads=scb, writes=[b_sm])
                k.op("dve", lambda scL=scL: nc.vector.tensor_reduce(out=mn, in_=scL, axis=AX.X, op=ALU.min), reads=scb, writes=[b_sm])
                k.op("dve", lambda: nc.vector.tensor_scalar(out=lo, in0=mn, scalar1=-1.0, scalar2=None, op0=ALU.add), reads=[b_sm], writes=[b_sm])
                k.op("dve", lambda: nc.vector.tensor_copy(out=hi, in_=mx), reads=[b_sm], writes=[b_sm])
                k.op("dve", lambda: nc.vector.tensor_tensor(out=sc[:, 0:1024], in0=sc[:, 0:1024], in1=padmask, op=ALU.add),
                     reads=[b_sc[0], b_sc[1], b_ac], writes=[b_sc[0], b_sc[1]])
                k.op("dve", lambda L=L: nc.vector.tensor_tensor(out=sc[:, L - 128:L], in0=sc[:, L - 128:L], in1=cmask, op=ALU.add),
                     reads=[b_sc[2 * j + 1], b_ac], writes=[b_sc[2 * j + 1]])
                for it in range(NBIS):
                    k.op("dve", lambda: nc.vector.scalar_tensor_tensor(out=mid, in0=lo, scalar=hi, in1=half, op0=ALU.add, op1=ALU.mult),
                         reads=[b_sm], writes=[b_sm])
                    k.op("dve", lambda scL=scL, L=L: nc.vector.tensor_scalar(
                        out=negm[:, :L], in0=scL, scalar1=mid, scalar2=0.0, op0=ALU.is_ge, op1=ALU.add, accum_out=cnt),
                        reads=scb + [b_sm], writes=[b_negm, b_sm])
                    k.op("dve", lambda: nc.vector.tensor_scalar(out=ge, in0=cnt, scalar1=TOPK - 0.5, scalar2=None, op0=ALU.is_ge),
                         reads=[b_sm], writes=[b_sm])
                    k.op("dve", lambda: nc.vector.tensor_tensor(out=d1, in0=mid, in1=lo, op=ALU.subtract), reads=[b_sm], writes=[b_sm])
                    k.op("dve", lambda: nc.vector.scalar_tensor_tensor(out=lo, in0=d1, scalar=ge, in1=lo, op0=ALU.mult, op1=ALU.add),
                         reads=[b_sm], writes=[b_sm])
                    k.op("dve", lambda: nc.vector.tensor_tensor(out=d1, in0=hi, in1=mid, op=ALU.subtract), reads=[b_sm], writes=[b_sm])
                    k.op("dve", lambda: nc.vector.scalar_tensor_tensor(out=hi, in0=d1, scalar=ge, in1=mid, op0=ALU.mult, op1=ALU.add),
                         reads=[b_sm], writes=[b_sm])
                k.op("dve", lambda scL=scL, L=L: nc.vector.tensor_scalar(
                    out=negm[:, :L], in0=scL, scalar1=lo, scalar2=NEG, op0=ALU.is_lt, op1=ALU.mult),
                    reads=scb + [b_sm], writes=[b_negm])
                for g in range(4):
                    qg = qT[:, g * 512:(g + 1) * 512]
                    pend = None
                    for chn in range(j + 1):
                        kt, bkt = ktc.next()
                        vt, bvt = vc.next()
                        k.dma("sp", kt, KT_d[g, :, chn * 1024:(chn + 1) * 1024], reads=[b_KT], writes=[bkt])
                        k.dma("sp", vt, V_v[:, chn * 8:(chn + 1) * 8, g * 128:(g + 1) * 128], reads=[b_V], writes=[bvt])
                        for b in range(8):
                            kb = chn * 8 + b
                            pl = lrot.next()
                            k.op("pe", lambda pl=pl, kt=kt, b=b, qg=qg: nc.tensor.matmul(
                                bank(pl), kt[:, b * 128:(b + 1) * 128], qg, start=True, stop=False),
                                reads=[bkt, bq], writes=[b_ps[pl]], inc=False)
                            k.op("pe", lambda pl=pl, kb=kb: nc.tensor.matmul(
                                bank(pl), negm[:, kb * 128:(kb + 1) * 128], identrep, start=False, stop=True),
                                reads=[b_negm, b_ac], writes=[b_ps[pl]])
                            p_, bp = pp.next()
                            if kb >= nkb - 2:
                                off = 0 if kb == nkb - 1 else 1
                                tf, btf = tmpf.next()
                                bsl = biasn[:, (off * 16 + 4 * g) * 128:(off * 16 + 4 * g + 4) * 128]
                                k.op("dve", lambda pl=pl, tf=tf, bsl=bsl: nc.vector.scalar_tensor_tensor(
                                    out=tf, in0=bank(pl), scalar=ATT_SCALE, in1=bsl, op0=ALU.mult, op1=ALU.add),
                                    reads=[b_ps[pl], b_ac], writes=[btf])
                                k.op("act", lambda tf=tf, p_=p_: nc.scalar.activation(out=p_, in_=tf, func=AF.Exp),
                                     reads=[btf], writes=[bp])
                            else:
                                k.op("act", lambda pl=pl, p_=p_: nc.scalar.activation(out=p_, in_=bank(pl), func=AF.Exp, scale=ATT_SCALE),
                                     reads=[b_ps[pl]], writes=[bp])
                            if pend is not None:
                                pend()
                            def pv(vt=vt, b=b, p_=p_, bp=bp, bvt=bvt, kb=kb):
                                k.op("pe", lambda: nc.tensor.matmul(bank(4), vt[:, b, :], p_, start=(kb == 0), stop=(kb == nkb - 1)),
                                     reads=[bvt, bp], writes=[b_ps[4]], inc=False)
                                k.op("pe", lambda: nc.tensor.matmul(bank(5), ones_b, p_, start=(kb == 0), stop=(kb == nkb - 1)),
                                     reads=[b_const, bp], writes=[b_ps[5]])
                            pend = pv
                    pend()
                    k.op("dve", lambda: nc.vector.reciprocal(out=rs, in_=bank(5)), reads=[b_ps[5]], writes=[b_rs])
                    o_, bo = oo.next()
                    k.op("dve", lambda o_=o_: nc.vector.tensor_tensor(out=o_, in0=bank(4), in1=rs, op=ALU.mult),
                         reads=[b_ps[4], b_rs], writes=[bo])
                    k.dma("sp", OT_v[:, 4 * g:4 * g + 4, j * 128:(j + 1) * 128], o_.rearrange("p (h t) -> p h t", h=4),
                          reads=[bo], writes=[b_OT])
            k.barrier()

    def phase_C():
        with ExitStack() as es:
            yain = A(es, "yain", [128, 16, 1024], BF16)
            ot = A(es, "ot", [128, 16, 1024], BF16)
            b_yain, b_ot = Buf("yain"), Buf("ot")
            for c0 in range(0, 16, 4):
                k.dma("sp", yain[:, c0:c0 + 4, :], YAIN_d[c0 * 128:(c0 + 4) * 128, :].rearrange("(c p) n -> p c n", p=128),
                      reads=[b_YAIN], writes=[b_yain])
                k.dma("sp", ot[:, c0:c0 + 4, :], OT_d[c0 * 128:(c0 + 4) * 128, :].rearrange("(c p) n -> p c n", p=128),
                      reads=[b_OT], writes=[b_ot])
            wa = mkrot(es, "wca", 2, [128, 16, 256], BF16)
            wb = mkrot(es, "wba", 2, [128, 16, 256], BF16)
            sga = mkrot(es, "sga", 2, [128, 1024], F32)
            sgb = mkrot(es, "sgb", 2, [128, 1024], F32)
            mo = mkrot(es, "mo", 2, [128, 1024], BF16)
            prs = Rot([(0, 2), (4, 6)])
            for t in range(16):
                wa3, bwa = wa.next()
                wb3, bwb = wb.next()
                k.dma("pool", wa3, wview(w_ca, 0, 2048, t * 256, 256), writes=[bwa])
                k.dma("pool", wb3, wview(w_ba, 0, 2048, t * 256, 256), writes=[bwb])
                for cch in range(2):
                    oc = t * 2 + cch
                    pa, pb = prs.next()
                    sa, bsa = sga.next()
                    sb_, bsb = sgb.next()
                    k.dma("sp", sa, SGA_d[oc * 128:(oc + 1) * 128, :], reads=[b_SGA], writes=[bsa])
                    k.dma("sp", sb_, SGB_d[oc * 128:(oc + 1) * 128, :], reads=[b_SGB], writes=[bsb])
                    proj(wa3, bwa, 16, cch * 128, lambda kc, o, n: yain[:, kc, o:o + n], [b_yain], NT_OWN, (pa, pa + 1))
                    proj(wb3, bwb, 16, cch * 128, lambda kc, o, n: ot[:, kc, o:o + n], [b_ot], NT_OWN, (pb, pb + 1))
                    k.op("dve", lambda sa=sa, pa=pa: nc.vector.tensor_tensor(out=sa, in0=pair(pa), in1=sa, op=ALU.mult),
                         reads=[b_ps[pa], b_ps[pa + 1], bsa], writes=[bsa])
                    k.op("dve", lambda sb_=sb_, pb=pb: nc.vector.tensor_tensor(out=sb_, in0=pair(pb), in1=sb_, op=ALU.mult),
                         reads=[b_ps[pb], b_ps[pb + 1], bsb], writes=[bsb])
                    m_, bm = mo.next()
                    k.op("dve", lambda m_=m_, sa=sa, sb_=sb_: nc.vector.tensor_tensor(out=m_, in0=sa, in1=sb_, op=ALU.add),
                         reads=[bsa, bsb], writes=[bm])
                    k.dma("sp", M_d[oc * 128:(oc + 1) * 128, :], m_, reads=[bm], writes=[b_M])
            k.barrier()

    def resid_proj(es, src3, b_src, nk, wd, xin_fn, xin_buf, xout_d, b_xo, wshape_cols=256):
        wr = mkrot(es, "wrp", 2, [128, nk, wshape_cols], BF16)
        xr = mkrot(es, "xrp", 3, [128, 1024], F32)
        prs = Rot([0, 2, 4, 6])
        for t in range(D // wshape_cols):
            w3, bw = wr.next()
            k.dma("pool", w3, wview(wd, 0, nk * 128, t * wshape_cols, wshape_cols), writes=[bw])
            for cch in range(wshape_cols // 128):
                oc = t * (wshape_cols // 128) + cch
                pb = prs.next()
                x_, bx = xr.next()
                k.dma("sp", x_, xin_fn(oc), reads=[xin_buf] if xin_buf is not None else [], writes=[bx])
                proj(w3, bw, nk, cch * 128, lambda kc, o, n: src3[:, kc, o:o + n], [b_src], NT_OWN, (pb, pb + 1))
                k.op("dve", lambda x_=x_, pb=pb: nc.vector.tensor_tensor(out=x_, in0=pair(pb), in1=x_, op=ALU.add),
                     reads=[b_ps[pb], b_ps[pb + 1], bx], writes=[bx])
                k.dma("sp", xout_d[oc * 128:(oc + 1) * 128, :], x_, reads=[bx], writes=[b_xo])

    def phase_D():
        with ExitStack() as es:
            mg = A(es, "mg", [128, 32, 1024], BF16)
            b_mg = Buf("mg")
            for c0 in range(0, 32, 4):
                k.dma("sp", mg[:, c0:c0 + 4, :], M_d[c0 * 128:(c0 + 4) * 128, :].rearrange("(c p) n -> p c n", p=128),
                      reads=[b_M], writes=[b_mg])
            resid_proj(es, mg, b_mg, 32, w_mo, lambda oc: xo[oc * 128:(oc + 1) * 128, 0:1024], None, X1_d, b_X1)
            k.barrier()

    def phase_E():
        with ExitStack() as es:
            hx = A(es, "hx", [128, 32, 1024], BF16)
            hm = A(es, "hm", [128, 32, 256], BF16)
            b_hx, b_hm = Buf("hx"), Buf("hm")
            qx = A(es, "qx", [128, 4, 1024], BF16)
            kx = A(es, "kx", [128, 4, 256], BF16)
            vx = A(es, "vx", [128, 2, 512], BF16)
            ox = A(es, "ox", [128, 4, 1024], BF16)
            b_qx, b_kx, b_vx, b_ox = Buf("qx"), Buf("kx"), Buf("vx"), Buf("ox")
            with ExitStack() as es2:
                res = dict(xc=mkrot(es2, "xce", 3, [128, 1024], F32), sq=mkrot(es2, "sqe", 2, [128, 1024], F32),
                           rstd=(A(es2, "rstde", [128, 1024], F32), Buf("rstde")))
                rmsnorm_T(X1_d, 1024, 1, res, (5, 6), dst3=hx, dst_buf=b_hx, src_buf=b_X1)
                rmsnorm_T(memT, 256, 2, res, (7,), dst3=hm, dst_buf=b_hm)
                k.barrier()
            wr = mkrot(es, "wxe", 2, [128, 32, 256], BF16)
            prs = Rot([0, 2])
            for t in range(2):
                w3, bw = wr.next()
                k.dma("pool", w3, wview(w_xq, 0, D, t * 256, 256), writes=[bw])
                for cch in range(2):
                    pb = prs.next()
                    proj(w3, bw, 32, cch * 128, lambda kc, o, n: hx[:, kc, o:o + n], [b_hx], NT_OWN, (pb, pb + 1))
                    copy_evac(alt(), qx[:, t * 2 + cch, :], pair(pb), [b_ps[pb], b_ps[pb + 1]], [b_qx])
            for t in range(2):
                w3, bw = wr.next()
                k.dma("pool", w3, wview(w_xkv, 0, D, t * 256, 256), writes=[bw])
                for cch in range(2):
                    pb = prs.next()
                    proj(w3, bw, 32, cch * 128, lambda kc, o, n: hm[:, kc, o:o + n], [b_hm], [(0, 256)], (pb,))
                    copy_evac(alt(), kx[:, t * 2 + cch, :], bank(pb, 256), [b_ps[pb]], [b_kx])
            for t in range(2):
                w3, bw = wr.next()
                k.dma("pool", w3, wview(w_xkv, 0, D, 512 + t * 256, 256), writes=[bw])
                for mc in range(2):
                    pb = prs.next()
                    for kc in range(32):
                        k.op("pe", lambda pb=pb, kc=kc, mc=mc, w3=w3: nc.tensor.matmul(
                            bank(pb, 256), hm[:, kc, mc * 128:(mc + 1) * 128], w3[:, kc, :], start=(kc == 0), stop=(kc == 31)),
                            reads=[bw, b_hm], writes=[b_ps[pb]], inc=(kc == 31))
                    copy_evac(alt(), vx[:, mc, t * 256:(t + 1) * 256], bank(pb, 256), [b_ps[pb]], [b_vx])
            pp = mkrot(es, "ppe", 4, [128, 512], BF16)
            rs = A(es, "rse", [128, 512], F32)
            b_rs = Buf("rse")
            lrot = Rot([0, 1, 2, 3])
            for h in range(4):
                for tt in range(2):
                    ps_ = []
                    for mc in range(2):
                        pl = lrot.next()
                        k.op("pe", lambda pl=pl, h=h, mc=mc, tt=tt: nc.tensor.matmul(
                            bank(pl), kx[:, h, mc * 128:(mc + 1) * 128], qx[:, h, tt * 512:(tt + 1) * 512], start=True, stop=True),
                            reads=[b_kx, b_qx], writes=[b_ps[pl]])
                        p_, bp = pp.next()
                        k.op("act", lambda pl=pl, p_=p_: nc.scalar.activation(out=p_, in_=bank(pl), func=AF.Exp, scale=ATT_SCALE),
                             reads=[b_ps[pl]], writes=[bp])
                        ps_.append((p_, bp))
                    for mc in range(2):
                        p_, bp = ps_[mc]
                        k.op("pe", lambda p_=p_, mc=mc, h=h: nc.tensor.matmul(
                            bank(4), vx[:, mc, h * 128:(h + 1) * 128], p_, start=(mc == 0), stop=(mc == 1)),
                            reads=[b_vx, bp], writes=[b_ps[4]], inc=False)
                        k.op("pe", lambda p_=p_, mc=mc: nc.tensor.matmul(bank(5), ones_b, p_, start=(mc == 0), stop=(mc == 1)),
                             reads=[b_const, bp], writes=[b_ps[5]])
                    k.op("dve", lambda: nc.vector.reciprocal(out=rs, in_=bank(5)), reads=[b_ps[5]], writes=[b_rs])
                    k.op("dve", lambda h=h, tt=tt: nc.vector.tensor_tensor(
                        out=ox[:, h, tt * 512:(tt + 1) * 512], in0=bank(4), in1=rs, op=ALU.mult),
                        reads=[b_ps[4], b_rs], writes=[b_ox])
            resid_proj(es, ox, b_ox, 4, w_xo, lambda oc: X1_d[oc * 128:(oc + 1) * 128, :], b_X1, X2_d, b_X2)
            k.barrier()

    def phase_F():
        with ExitStack() as es:
            h2 = A(es, "h2", [128, 32, 1024], BF16)
            b_h2 = Buf("h2")
            with ExitStack() as es2:
                res = dict(xc=mkrot(es2, "xcf", 3, [128, 1024], F32), sq=mkrot(es2, "sqf", 2, [128, 1024], F32),
                           rstd=(A(es2, "rstdf", [128, 1024], F32), Buf("rstdf")))
                rmsnorm_T(X2_d, 1024, 3, res, (6, 7), dst3=h2, dst_buf=b_h2, src_buf=b_X2)
                k.barrier()
            wr = mkrot(es, "wup", 3, [128, 32, 256], BF16)
            rl = mkrot(es, "rl", 2, [128, 1024], F32)
            ao = mkrot(es, "ao", 3, [128, 1024], BF16)
            prs = Rot([0, 2, 4])
            for t in range(DFF // 256):
                w3, bw = wr.next()
                k.dma("pool", w3, wview(w_up, 0, D, t * 256, 256), writes=[bw])
                for cch in range(2):
                    fc = t * 2 + cch
                    pb = prs.next()
                    proj(w3, bw, 32, cch * 128, lambda kc, o, n: h2[:, kc, o:o + n], [b_h2], NT_OWN, (pb, pb + 1))
                    r_, br = rl.next()
                    k.op("act", lambda r_=r_, pb=pb: nc.scalar.activation(out=r_, in_=pair(pb), func=AF.Relu),
                         reads=[b_ps[pb], b_ps[pb + 1]], writes=[br])
                    a_, ba = ao.next()
                    k.op("dve", lambda r_=r_, a_=a_: nc.vector.tensor_tensor(out=a_, in0=r_, in1=r_, op=ALU.mult),
                         reads=[br], writes=[ba])
                    k.dma("sp", A_d[fc * 128:(fc + 1) * 128, :], a_, reads=[ba], writes=[b_A])
            k.barrier()
        with ExitStack() as es:
            ah = A(es, "ah", [128, 64, 1024], BF16)
            b_ah = Buf("ah")
            wr = mkrot(es, "wdn", 2, [128, 64, 128], BF16)
            xr = mkrot(es, "xrf", 2, [128, 1024], F32)
            pr = mkrot(es, "prf", 2, [128, 1024], F32)
            prs = Rot([0, 2, 4, 6])
            for kh in range(2):
                for c0 in range(0, 64, 8):
                    r0 = kh * 8192 + c0 * 128
                    k.dma("sp", ah[:, c0:c0 + 8, :], A_d[r0:r0 + 1024, :].rearrange("(c p) n -> p c n", p=128),
                          reads=[b_A], writes=[b_ah])
                for oc in range(32):
                    w3, bw = wr.next()
                    k.dma("pool", w3, wview(w_dn, kh * 8192, 8192, oc * 128, 128), writes=[bw])
                    pb = prs.next()
                    proj(w3, bw, 64, 0, lambda kc, o, n: ah[:, kc, o:o + n], [b_ah], NT_OWN, (pb, pb + 1))
                    p_, bp = pr.next()
                    if kh == 0:
                        copy_evac(alt(), p_, pair(pb), [b_ps[pb], b_ps[pb + 1]], [bp])
                        k.dma("sp", P_d[oc * 128:(oc + 1) * 128, :], p_, reads=[bp], writes=[b_P])
                    else:
                        x_, bx = xr.next()
                        k.dma("sp", p_, P_d[oc * 128:(oc + 1) * 128, :], reads=[b_P], writes=[bp])
                        k.dma("sp", x_, X2_d[oc * 128:(oc + 1) * 128, :], reads=[b_X2], writes=[bx])
                        k.op("dve", lambda p_=p_, pb=pb: nc.vector.tensor_tensor(out=p_, in0=pair(pb), in1=p_, op=ALU.add),
                             reads=[b_ps[pb], b_ps[pb + 1], bp], writes=[bp])
                        k.op("dve", lambda p_=p_, x_=x_: nc.vector.tensor_tensor(out=x_, in0=p_, in1=x_, op=ALU.add),
                             reads=[bp, bx], writes=[bx])
                        k.dma("sp", xout[oc * 128:(oc + 1) * 128, :], x_, reads=[bx], writes=[b_xout])
            k.barrier()

    def phase_G():
        with ExitStack() as es:
            res = dict(xc=mkrot(es, "xcg", 3, [128, 1024], F32), sq=mkrot(es, "sqg", 2, [128, 1024], F32),
                       rstd=(A(es, "rstdg", [128, 1024], F32), Buf("rstdg")), of=mkrot(es, "ofg", 3, [128, 1024], F32))
            rmsnorm_T(xout, 1024, 4, res, (6, 7), src_buf=b_xout, out_d=yn, out_buf=b_yn)
            k.barrier()

    phases = dict(K=phase_K, P=phase_P, A=phase_A, C=phase_C, D=phase_D, E=phase_E, F=phase_F, G=phase_G)
    for ph in PHASES:
        phases[ph]()
    k.barrier()
    return nc, k


_PROG = {}


def _get_prog(phases="KPACDEFG", debug_out=()):
    key = (phases, tuple(debug_out))
    if key not in _PROG:
        _PROG[key] = build_program(phases, debug_out)
    return _PROG[key]


def _t5_bucket_np(dist):
    n = np.maximum(dist, 0)
    nf = np.maximum(n, 1).astype(np.float32)
    large = 16 + (np.log(nf / np.float32(16)) / np.float32(math.log(128 / 16)) * np.float32(16)).astype(np.int32)
    large = np.minimum(large, 31)
    return np.where(n < 16, n, large)


def _own_idx(c):
    return np.concatenate([np.arange((8 * j + c) * 128, (8 * j + c + 1) * 128) for j in range(8)])


def _consts(rel_bias):
    rel_bias = np.asarray(rel_bias, np.float32)
    s = np.arange(128)[:, None]
    t = np.arange(128)[None, :]
    biasn = np.zeros((128, 2, 16, 128), np.float32)
    for off in range(2):
        bk = _t5_bucket_np(t - s + 128 * off)
        biasn[:, off, :, :] = rel_bias[bk].transpose(0, 2, 1)
    bfar = np.ascontiguousarray(np.broadcast_to(rel_bias[31][None, :], (128, 16))).astype(np.float32)
    identrep = np.ascontiguousarray(np.tile(np.eye(128, dtype=np.float32), (1, 4)))
    tt = np.arange(128)[:, None]
    ss = np.arange(128)[None, :]
    cmask = np.where(ss > tt, np.float32(BIG), np.float32(0)).astype(np.float32)
    return dict(biasn=biasn.reshape(128, -1), bfar=bfar, identrep=identrep, cmask=cmask)


def _layer_in_maps(l, xfull, inp, consts, names):
    xT = np.ascontiguousarray(xfull.T)
    g = np.stack([inp["norm_mix"][l], inp["norm_xattn"][l], inp["norm_mem"][l], inp["norm_mlp"][l], inp["norm_final"]])
    gains = np.ascontiguousarray(g.reshape(5, 32, 128).transpose(2, 0, 1).reshape(128, 160)).astype(np.float32)
    convw = np.ascontiguousarray(inp["conv_w"][l].reshape(3, 16, 128).transpose(2, 1, 0).reshape(128, 48)).astype(np.float32)
    shared = dict(
        w_in=inp["w_in"][l], convw=convw, w_ca=inp["w_conv_out"][l], w_ba=inp["w_attn_out"][l],
        w_mo=inp["w_mix_out"][l], w_xq=inp["w_xq"][l], w_xkv=inp["w_xkv"][l], w_xo=inp["w_xo"][l],
        w_up=inp["w_up"][l], w_dn=inp["w_down"][l], gains=gains,
        memT=np.ascontiguousarray(inp["mem"][0].T), **consts)
    maps = []
    for c in range(NCORE):
        m = {}
        if "xo" in names:
            xo = np.zeros((D, NH), np.float32)
            xo[:, :NOWN] = xT[:, _own_idx(c)]
            for j in range(8):
                t0 = (8 * j + c) * 128
                if t0 >= 2:
                    xo[:, NOWN + 2 * j:NOWN + 2 * j + 2] = xT[:, t0 - 2:t0]
            m["xo"] = xo
        if "xall" in names:
            xa = np.zeros((D, S), np.float32)
            sh = (7 - c) * 128
            xa[:, sh:] = xT[:, :S - sh]
            m["xall"] = xa
        if "padmask" in names:
            pm = np.zeros((128, 1024), np.float32)
            pm[:, :(7 - c) * 128] = BIG
            m["padmask"] = pm
        for n in names:
            if n not in m:
                m[n] = np.ascontiguousarray(np.asarray(shared[n], np.float32))
        maps.append(m)
    return maps


def kernel(**inputs):
    inp = {k_: np.asarray(v) for k_, v in inputs.items()}
    nc, kb = _get_prog()
    names = kb.in_names
    consts = _consts(inp["rel_bias"])
    x = np.asarray(inp["x"][0], np.float32)
    out = None
    for l in range(2):
        maps = _layer_in_maps(l, x, inp, consts, names)
        res = run_bass_kernel_spmd(nc, maps, core_ids=list(range(NCORE)))
        del maps
        key = "xout" if l == 0 else "yn"
        nxt = np.empty((S, D), np.float32)
        for c in range(NCORE):
            nxt[_own_idx(c)] = res.results[c][key].T
        x = nxt
    return x[None].astype(np.float32)
```
